# Optimizing a Trainium2 kernel written in Bass

```python
import math
import jax, jax.numpy as jnp
from jax import lax
import numpy as np

D_MODEL = 1024
BATCH = 4
SEQ = 4096
DEPTH = 4

N_MEM = 256
XA_HEADS = 4
XA_HEAD_DIM = D_MODEL // XA_HEADS
HEAD_DIM = 64
DIL_GROUPS = ((128, 1), (512, 4), (2048, 16))
N_DIL = len(DIL_GROUPS)
HEADS_PER_GROUP = 8
ATT_WIDTH = N_DIL * HEADS_PER_GROUP * HEAD_DIM
ATT_OUT = HEADS_PER_GROUP * HEAD_DIM
SSD_EXPAND = 2
SSD_INNER = SSD_EXPAND * D_MODEL
SSD_HEAD_DIM = 64
SSD_HEADS = SSD_INNER // SSD_HEAD_DIM
SSD_STATE = 128
SSD_GROUPS = 8
SSD_CONV = 4
SSD_CHUNK = 128
SSD_CONV_CH = SSD_INNER + 2 * SSD_GROUPS * SSD_STATE
N_BRANCH = 2
IN_WIDTH = 3 * ATT_WIDTH + SSD_INNER + SSD_CONV_CH + SSD_HEADS + N_BRANCH * D_MODEL
D_FF = 2816
N_SUBNORMS = 8
EPS = 1e-6

kernel_name = "hybrid_dilated_attn_ssd_macaron_trunk"


def rmsnorm(x, g):
    xf = x.astype(jnp.float32)
    y = xf * lax.rsqrt(jnp.mean(xf * xf, axis=-1, keepdims=True) + EPS)
    return (y * g.astype(jnp.float32)).astype(x.dtype)


def swiglu(x, w_gate, w_up, w_down):
    return (jax.nn.silu(x @ w_gate) * (x @ w_up)) @ w_down


def dilated_group_attention(q, k, v, window, dilation):
    b, s, h, hd = q.shape
    band = window // dilation
    blk = band
    L = s // dilation
    nb = -(-L // blk)
    Lp = nb * blk

    def to_sub(t):
        t = t.reshape(b, L, dilation, h, hd).transpose(0, 2, 3, 1, 4)
        return jnp.pad(t, ((0, 0), (0, 0), (0, 0), (0, Lp - L), (0, 0)))

    def band_keys(t):
        tb = t.reshape(b, dilation, h, nb, blk, hd)
        prev = jnp.pad(tb, ((0, 0), (0, 0), (0, 0), (1, 0), (0, 0), (0, 0)))[:, :, :, :nb]
        return jnp.concatenate([prev, tb], axis=4)

    qb = to_sub(q).reshape(b, dilation, h, nb, blk, hd)
    kb = band_keys(to_sub(k))
    vb = band_keys(to_sub(v))
    scores = jnp.einsum('brhnqd,brhnkd->brhnqk', qb, kb).astype(jnp.float32) * (hd ** -0.5)
    qpos = jnp.arange(nb)[:, None, None] * blk + jnp.arange(blk)[None, :, None]
    kpos = jnp.arange(nb)[:, None, None] * blk - blk + jnp.arange(2 * blk)[None, None, :]
    diff = qpos - kpos
    mask = (diff >= 0) & (diff <= band) & (kpos >= 0)
    scores = jnp.where(mask, scores, -jnp.inf)
    m = jnp.max(scores, axis=-1, keepdims=True)
    p = jnp.exp(scores - m)
    denom = jnp.sum(p, axis=-1, keepdims=True)
    out = jnp.einsum('brhnqk,brhnkd->brhnqd', (p / denom).astype(v.dtype), vb)
    lse = (m + jnp.log(denom))[..., 0]
    out = out.reshape(b, dilation, h, Lp, hd)[:, :, :, :L].transpose(0, 3, 1, 2, 4).reshape(b, s, h, hd)
    lse = lse.reshape(b, dilation, h, Lp)[:, :, :, :L].transpose(0, 3, 1, 2).reshape(b, s, h)
    return out, lse


def dilated_attention(q, k, v):
    outs, lses = [], []
    for g, (w, d) in enumerate(DIL_GROUPS):
        o, l = dilated_group_attention(q[:, :, g], k[:, :, g], v[:, :, g], w, d)
        outs.append(o)
        lses.append(l)
    o = jnp.stack(outs, axis=2)
    alpha = jax.nn.softmax(jnp.stack(lses, axis=2), axis=2)
    return jnp.einsum('bsgh,bsghd->bshd', alpha.astype(o.dtype), o)


def causal_depthwise_conv(x, w, bias):
    K, C = w.shape
    y = lax.conv_general_dilated(x, w[:, None, :].astype(x.dtype), window_strides=(1,),
                                 padding=[(K - 1, 0)], dimension_numbers=('NWC', 'WIO', 'NWC'),
                                 feature_group_count=C)
    return y + bias


def ssd_chunked(x, dt, A, Bm, Cm):
    b, s, nh, P = x.shape
    G, N = Bm.shape[2], Bm.shape[3]
    R = nh // G
    Q = SSD_CHUNK
    c = s // Q
    xdt = (x * dt[..., None]).reshape(b, c, Q, G, R, P)
    a = (dt * A).reshape(b, c, Q, G, R).transpose(0, 1, 3, 4, 2)
    Bc = Bm.reshape(b, c, Q, G, N)
    Cc = Cm.reshape(b, c, Q, G, N)
    a_cs = jnp.cumsum(a, axis=-1)
    seg = a_cs[..., :, None] - a_cs[..., None, :]
    causal = jnp.tril(jnp.ones((Q, Q), dtype=bool))
    Lmat = jnp.exp(jnp.where(causal, seg, -jnp.inf))
    CB = jnp.einsum('bclgn,bcsgn->bcgls', Cc, Bc)
    Wdiag = CB[:, :, :, None] * Lmat
    y_diag = jnp.einsum('bcgrls,bcsgrp->bclgrp', Wdiag, xdt)
    decay_states = jnp.exp(a_cs[..., -1:] - a_cs).transpose(0, 1, 4, 2, 3)
    states = jnp.einsum('bclgn,bclgrp->bcgrpn', Bc, xdt * decay_states[..., None])
    chunk_decay = jnp.exp(a_cs[..., -1])

    def step(hstate, inp):
        st, dec = inp
        return hstate * dec[..., None, None] + st, hstate

    h0 = jnp.zeros((b, G, R, P, N), dtype=states.dtype)
    _, prev = lax.scan(step, h0, (states.transpose(1, 0, 2, 3, 4, 5), chunk_decay.transpose(1, 0, 2, 3)))
    prev = prev.transpose(1, 0, 2, 3, 4, 5)
    decay_out = jnp.exp(a_cs).transpose(0, 1, 4, 2, 3)
    y_off = jnp.einsum('bclgn,bcgrpn->bclgrp', Cc, prev) * decay_out[..., None]
    return (y_diag + y_off).reshape(b, s, nh, P)


def ssd_branch(z, xBC, dt_raw, conv_w, conv_b, dt_bias, a_log, d_skip, norm_g):
    b, s, _ = z.shape
    xBC = jax.nn.silu(causal_depthwise_conv(xBC, conv_w, conv_b))
    xs = xBC[..., :SSD_INNER].reshape(b, s, SSD_HEADS, SSD_HEAD_DIM)
    Bm = xBC[..., SSD_INNER:SSD_INNER + SSD_GROUPS * SSD_STATE].reshape(b, s, SSD_GROUPS, SSD_STATE)
    Cm = xBC[..., SSD_INNER + SSD_GROUPS * SSD_STATE:].reshape(b, s, SSD_GROUPS, SSD_STATE)
    dt = jax.nn.softplus(dt_raw.astype(jnp.float32) + dt_bias.astype(jnp.float32))
    A = -jnp.exp(a_log.astype(jnp.float32))
    y = ssd_chunked(xs, dt, A, Bm, Cm) + d_skip.astype(jnp.float32)[:, None] * xs
    y = (y.reshape(b, s, SSD_INNER).astype(z.dtype)) * jax.nn.silu(z)
    yg = rmsnorm(y.reshape(b, s, SSD_GROUPS, SSD_INNER // SSD_GROUPS),
                 jnp.ones((SSD_INNER // SSD_GROUPS,), dtype=y.dtype))
    return yg.reshape(b, s, SSD_INNER) * norm_g


def memory_cross_attention(u, mem_n, wq, wk, wv, wo):
    b, s, _ = u.shape
    m = mem_n.shape[1]
    q = (u @ wq).reshape(b, s, XA_HEADS, XA_HEAD_DIM)
    k = (mem_n @ wk).reshape(b, m, XA_HEADS, XA_HEAD_DIM)
    v = (mem_n @ wv).reshape(b, m, XA_HEADS, XA_HEAD_DIM)
    sc = jnp.einsum('bshd,bmhd->bhsm', q, k).astype(jnp.float32) * (XA_HEAD_DIM ** -0.5)
    p = jax.nn.softmax(sc, axis=-1).astype(v.dtype)
    o = jnp.einsum('bhsm,bmhd->bshd', p, v).reshape(b, s, XA_HEADS * XA_HEAD_DIM)
    return o @ wo


def setup_inputs(seed: int = 0) -> dict:
    key = jax.random.key(seed)
    ks = jax.random.split(key, 32)
    f32 = jnp.float32

    def nrm(k, shape, fan_in):
        return jax.random.normal(k, shape, f32) * (fan_in ** -0.5)

    dt0 = jnp.exp(jax.random.uniform(ks[10], (DEPTH, SSD_HEADS), f32, math.log(1e-3), math.log(1e-1)))
    return {
        "x": jax.random.normal(ks[0], (BATCH, SEQ, D_MODEL), f32),
        "mem": jax.random.normal(ks[1], (BATCH, N_MEM, D_MODEL), f32),
        "norm_g": 1.0 + 0.02 * jax.random.normal(ks[2], (DEPTH, N_SUBNORMS, D_MODEL), f32),
        "ffn1_gate": nrm(ks[3], (DEPTH, D_MODEL, D_FF), D_MODEL),
        "ffn1_up": nrm(ks[4], (DEPTH, D_MODEL, D_FF), D_MODEL),
        "ffn1_down": nrm(ks[5], (DEPTH, D_FF, D_MODEL), D_FF),
        "w_in": nrm(ks[6], (DEPTH, D_MODEL, IN_WIDTH), D_MODEL),
        "b_gate": 0.01 * jax.random.normal(ks[7], (DEPTH, N_BRANCH * D_MODEL), f32),
        "conv_w": nrm(ks[8], (DEPTH, SSD_CONV, SSD_CONV_CH), SSD_CONV),
        "conv_b": 0.01 * jax.random.normal(ks[9], (DEPTH, SSD_CONV_CH), f32),
        "dt_bias": dt0 + jnp.log(-jnp.expm1(-dt0)),
        "a_log": jnp.log(jax.random.uniform(ks[11], (DEPTH, SSD_HEADS), f32, 1.0, 16.0)),
        "d_skip": 1.0 + 0.02 * jax.random.normal(ks[12], (DEPTH, SSD_HEADS), f32),
        "ssd_norm_g": 1.0 + 0.02 * jax.random.normal(ks[13], (DEPTH, SSD_INNER), f32),
        "w_att_out": nrm(ks[14], (DEPTH, ATT_OUT, D_MODEL), ATT_OUT),
        "w_ssd_out": nrm(ks[15], (DEPTH, SSD_INNER, D_MODEL), SSD_INNER),
        "w_o": nrm(ks[16], (DEPTH, D_MODEL, D_MODEL), D_MODEL),
        "mem_norm_g": 1.0 + 0.02 * jax.random.normal(ks[17], (DEPTH, D_MODEL), f32),
        "xa_wq": nrm(ks[18], (DEPTH, D_MODEL, XA_HEADS * XA_HEAD_DIM), D_MODEL),
        "xa_wk": nrm(ks[19], (DEPTH, D_MODEL, XA_HEADS * XA_HEAD_DIM), D_MODEL),
        "xa_wv": nrm(ks[20], (DEPTH, D_MODEL, XA_HEADS * XA_HEAD_DIM), D_MODEL),
        "xa_wo": nrm(ks[21], (DEPTH, XA_HEADS * XA_HEAD_DIM, D_MODEL), XA_HEADS * XA_HEAD_DIM),
        "ffn2_gate": nrm(ks[22], (DEPTH, D_MODEL, D_FF), D_MODEL),
        "ffn2_up": nrm(ks[23], (DEPTH, D_MODEL, D_FF), D_MODEL),
        "ffn2_down": nrm(ks[24], (DEPTH, D_FF, D_MODEL), D_FF),
    }


def reference(x, mem, norm_g, ffn1_gate, ffn1_up, ffn1_down, w_in, b_gate, conv_w, conv_b,
              dt_bias, a_log, d_skip, ssd_norm_g, w_att_out, w_ssd_out, w_o, mem_norm_g,
              xa_wq, xa_wk, xa_wv, xa_wo, ffn2_gate, ffn2_up, ffn2_down):
    b, s, _ = x.shape
    sizes = (ATT_WIDTH, ATT_WIDTH, ATT_WIDTH, SSD_INNER, SSD_CONV_CH, SSD_HEADS, N_BRANCH * D_MODEL)
    cuts = []
    acc = 0
    for sz in sizes[:-1]:
        acc += sz
        cuts.append(acc)
    h = x
    for l in range(DEPTH):
        g = norm_g[l]
        h = h + 0.5 * rmsnorm(swiglu(rmsnorm(h, g[0]), ffn1_gate[l], ffn1_up[l], ffn1_down[l]), g[1])
        u = rmsnorm(h, g[2])
        q, k, v, z, xBC, dt_raw, gate_pre = jnp.split(u @ w_in[l], cuts, axis=-1)
        hshape = (b, s, N_DIL, HEADS_PER_GROUP, HEAD_DIM)
        y_att = dilated_attention(q.reshape(hshape), k.reshape(hshape), v.reshape(hshape)).reshape(b, s, ATT_OUT)
        y_ssd = ssd_branch(z, xBC, dt_raw, conv_w[l], conv_b[l], dt_bias[l], a_log[l], d_skip[l], ssd_norm_g[l])
        gates = jax.nn.sigmoid(gate_pre + b_gate[l]).reshape(b, s, N_BRANCH, D_MODEL)
        merged = gates[:, :, 0] * (y_att @ w_att_out[l]) + gates[:, :, 1] * (y_ssd @ w_ssd_out[l])
        h = h + rmsnorm(merged @ w_o[l], g[3])
        mem_n = rmsnorm(mem, mem_norm_g[l])
        h = h + rmsnorm(memory_cross_attention(rmsnorm(h, g[4]), mem_n, xa_wq[l], xa_wk[l], xa_wv[l], xa_wo[l]), g[5])
        h = h + 0.5 * rmsnorm(swiglu(rmsnorm(h, g[6]), ffn2_gate[l], ffn2_up[l], ffn2_down[l]), g[7])
    return h
```

```python
from contextlib import ExitStack
import numpy as np
import ml_dtypes
import concourse.bass as bass
import concourse.mybir as mybir
from concourse.bass_utils import run_bass_kernel_spmd

F32 = mybir.dt.float32
BF16 = mybir.dt.bfloat16
AF = mybir.ActivationFunctionType
ALU = mybir.AluOpType
AX = mybir.AxisListType
NPBF = ml_dtypes.bfloat16

NCORES = 8
D = 1024
DFF = 2816
DEPTH = 4
BATCH = 4
SEQ = 4096
NMEM = 256
INW = 12832
EPS = 1e-6
FAKE_CONTIG = False
WORK_CORES = (0, 1, 4, 5)


class Buf:
    __slots__ = ("name", "w", "r")

    def __init__(self, name):
        self.name = name
        self.w = None
        self.r = []


class KB:
    def __init__(self):
        self.nc = bass.Bass("TRN2", target_bir_lowering=False)
        self.es = ExitStack()
        nc = self.nc
        self.eng = {"pe": nc.tensor, "act": nc.scalar, "dve": nc.vector, "pool": nc.gpsimd, "sp": nc.sync}
        self.sem = {}
        self.cnt = {}
        for e in self.eng:
            self.sem[e] = self.es.enter_context(nc.semaphore("s_" + e))
            self.cnt[e] = 0
        self.known = {e: {} for e in self.eng}
        self.sem_pool = []
        self.nps = 0
        self.pcnt = {}
        import threading
        self.tls = threading.local()
        self.default_banks = list(range(8))
        self.phase_id = 0
        self.hook = None
        self.pes = None
        self.ndram = 0
        self.begin_phase()

    def begin_phase(self):
        self.phase_id += 1
        self.pes = ExitStack()
        self.keymap = {}
        self.next_pool = 0

    def end_phase(self):
        self.barrier()
        self.pes.close()
        self.pes = None

    def sb(self, name, shape, dt):
        return self.pes.enter_context(self.nc.sbuf_tensor("p%d_%s" % (self.phase_id, name), list(shape), dt))

    def psum_banks(self):
        self.pbank = []
        for i in range(8):
            t = self.es.enter_context(self.nc.psum_tensor("psb%d" % i, [128, 512], F32))
            self.pbank.append((t, Buf("psb%d" % i)))

    def ps(self, pool=None):
        if pool is None:
            pool = getattr(self.tls, "pool", None)
        if pool is None:
            t, b = self.pbank[self.default_banks[self.nps % len(self.default_banks)]]
            self.nps += 1
            return t, b
        ids, key = pool
        n = self.pcnt.get(key, 0)
        self.pcnt[key] = n + 1
        return self.pbank[ids[n % len(ids)]]

    def barrier(self):
        for e in self.eng:
            for f in self.eng:
                if f != e and self.cnt[f] > self.known[e].get(f, 0):
                    self.eng[e].wait_ge(self.sem[f], self.cnt[f])
                    self.known[e][f] = self.cnt[f]
            for k, (s_, v) in enumerate(self.sem_pool):
                if v > self.known[e].get(k, 0):
                    self.eng[e].wait_ge(s_, v)
                    self.known[e][k] = v

    def din(self, name, shape, dt):
        return self.nc.dram_tensor(name, list(shape), dt, kind="ExternalInput").ap()

    def dout(self, name, shape, dt):
        return self.nc.dram_tensor(name, list(shape), dt, kind="ExternalOutput").ap()

    def dint(self, name, shape, dt):
        return self.nc.dram_tensor(name, list(shape), dt, kind="Internal").ap()

    def _sem_for(self, key):
        if key in self.eng:
            return self.sem[key]
        return self.sem_pool[key][0]

    def _waits(self, e, reads, writes):
        deps = {}
        for b in reads:
            if b.w is not None:
                k, v = b.w
                deps[k] = max(deps.get(k, 0), v)
        for b in writes:
            if b.w is not None:
                k, v = b.w
                deps[k] = max(deps.get(k, 0), v)
            for (k, v) in b.r:
                deps[k] = max(deps.get(k, 0), v)
        engine = self.eng[e]
        for k, v in deps.items():
            if k == e and e == "pe":
                continue
            if self.known[e].get(k, 0) >= v:
                continue
            engine.wait_ge(self._sem_for(k), v)
            self.known[e][k] = v

    def op(self, e, fn, reads=(), writes=()):
        self._waits(e, reads, writes)
        ins = fn(self.eng[e])
        self.cnt[e] += 1
        ins.then_inc(self.sem[e], 1)
        tok = (e, self.cnt[e])
        for b in reads:
            b.r.append(tok)
        for b in writes:
            b.w = tok
            b.r = []
        if self.hook is not None:
            self.hook()
        return ins

    def atomic(self):
        kb = self

        class _A:
            def __enter__(self_):
                self_.h = kb.hook
                kb.hook = None

            def __exit__(self_, *a):
                kb.hook = self_.h
        return _A()

    def _pool_idx(self, key):
        if key not in self.keymap:
            if self.next_pool >= len(self.sem_pool):
                s_ = self.es.enter_context(self.nc.semaphore("dq%d" % len(self.sem_pool)))
                self.sem_pool.append([s_, 0])
            self.keymap[key] = self.next_pool
            self.next_pool += 1
        return self.keymap[key]

    def dma(self, q, out, in_, reads=(), writes=(), key=None, **kw):
        return self.dma_multi(q, [(out, in_)], reads=reads, writes=writes, key=key, **kw)

    def dma_multi(self, q, pairs, reads=(), writes=(), key=None, **kw):
        self._waits(q, reads, writes)
        if key is None:
            key = "d_" + (writes[0].name if writes else reads[0].name)
        idx = self._pool_idx(key)
        ent = self.sem_pool[idx]
        for (out, in_) in pairs:
            ent[1] += 16
            self.eng[q].dma_start(out=out, in_=in_, **kw).then_inc(ent[0], 16)
        tok = (idx, ent[1])
        for b in reads:
            b.r.append(tok)
        for b in writes:
            b.w = tok
            b.r = []
        return tok

    def finish(self, out_toks=()):
        self.end_phase()
        self.es.close()
        return self.nc


def emit_cast(kb, x, y, N, CH=4096):
    nch = (N + CH - 1) // CH
    NB = 3
    xin = [(kb.sb("xin%d" % i, [128, CH], F32), Buf("xin%d" % i)) for i in range(NB)]
    xo = [(kb.sb("xo%d" % i, [128, CH], BF16), Buf("xo%d" % i)) for i in range(NB)]
    engs = ["dve", "act", "pool"]
    for c in range(nch):
        lo = c * CH
        w = min(CH, N - lo)
        ti, bi = xin[c % NB]
        to, bo = xo[c % NB]
        kb.dma("sp", ti[:, 0:w], x[:, lo:lo + w], writes=[bi])
        e = engs[c % 3]
        if e == "act":
            kb.op("act", lambda g: g.copy(out=to[:, 0:w], in_=ti[:, 0:w]), reads=[bi], writes=[bo])
        else:
            kb.op(e, lambda g: g.tensor_copy(out=to[:, 0:w], in_=ti[:, 0:w]), reads=[bi], writes=[bo])
        kb.dma("pool", y[:, lo:lo + w], to[:, 0:w], reads=[bo])


def build_cast(N):
    kb = KB()
    x = kb.din("x", [128, N], F32)
    y = kb.dout("y", [128, N], BF16)
    emit_cast(kb, x, y, N)
    return kb.finish()


_cache = {}


def run(nc, in_maps):
    res = run_bass_kernel_spmd(nc, in_maps, core_ids=list(range(NCORES)))
    return res.results


def cast_weights(arrs):
    flat = [np.ascontiguousarray(a, dtype=np.float32).reshape(-1) for a in arrs.values()]
    tot = sum(f.size for f in flat)
    per = NCORES * 128
    N = (tot + per - 1) // per
    N = (N + 63) // 64 * 64
    big = np.zeros(per * N, np.float32)
    big[:tot] = np.concatenate(flat)
    big = big.reshape(NCORES, 128, N)
    key = ("cast", N)
    if key not in _cache:
        _cache[key] = build_cast(N)
    res = run(_cache[key], [{"x": big[i]} for i in range(NCORES)])
    out = np.concatenate([np.asarray(r["y"]).reshape(-1) for r in res])
    outs = {}
    off = 0
    for name, a in arrs.items():
        outs[name] = out[off:off + a.size].reshape(a.shape)
        off += a.size
    return outs


def interleave(kb, fns):
    import threading
    fns = [f for f in fns if f is not None]
    if len(fns) == 1:
        fns[0]()
        return
    n = len(fns)
    sems = [threading.Semaphore(0) for _ in fns]
    done = [False] * n
    cur = [0]
    errs = []
    main = threading.Semaphore(0)

    def nxt(i):
        for k in range(1, n + 1):
            j = (i + k) % n
            if not done[j]:
                return j
        return None

    def hook():
        i = cur[0]
        j = nxt(i)
        if j is None or j == i:
            return
        cur[0] = j
        sems[j].release()
        sems[i].acquire()

    def runner(i):
        sems[i].acquire()
        try:
            fns[i]()
        except BaseException as e:
            errs.append(e)
        done[i] = True
        j = nxt(i)
        if j is not None:
            cur[0] = j
            sems[j].release()
        else:
            main.release()

    old = kb.hook
    kb.hook = hook
    ths = [threading.Thread(target=runner, args=(i,)) for i in range(n)]
    for t in ths:
        t.start()
    cur[0] = 0
    sems[0].release()
    main.acquire()
    for t in ths:
        t.join()
    kb.hook = old
    if errs:
        raise errs[0]


class SideCast:
    def __init__(self, kb, pieces, nseg):
        self.kb, self.pieces, self.k = kb, list(pieces), 0
        self.per = (len(self.pieces) + nseg - 1) // nseg if self.pieces else 0
        if self.pieces:
            CH = max(w_ for (_, _, w_) in self.pieces)
            self.cin = Rot(kb, "cin", [128, CH], F32, 3)
            self.cout = Rot(kb, "cout", [128, CH], BF16, 3)

    def next(self, last=False):
        items = self.pieces[self.k:] if last else self.pieces[self.k:self.k + self.per]
        self.k += len(items)
        if not items:
            return None
        kb = self.kb

        def run_():
            for (src, dst, w_) in items:
                ti, bi = self.cin.get()
                to, bo = self.cout.get()
                kb.dma("sp", ti[:, 0:w_], src, writes=[bi])
                kb.op("pool", lambda g_: g_.tensor_copy(out=to[:, 0:w_], in_=ti[:, 0:w_]), reads=[bi], writes=[bo])
                kb.dma("pool", dst, to[:, 0:w_], reads=[bo])
        return run_


class Rot:
    def __init__(self, kb, name, shape, dt, n):
        self.items = [(kb.sb("%s%d" % (name, i), shape, dt), Buf("%s%d" % (name, i))) for i in range(n)]
        self.i = 0

    def get(self):
        it = self.items[self.i % len(self.items)]
        self.i += 1
        return it


def kb_op_noinc(kb, e, fn, reads=(), writes=()):
    kb._waits(e, reads, writes)
    fn(kb.eng[e])
    tok = (e, kb.cnt[e] + 1)
    for b in reads:
        b.r.append(tok)
    for b in writes:
        b.w = tok
        b.r = []


def mm_group(kb, pst, pbuf, out_ap, pairs, reads):
    n = len(pairs)
    for i, (l, r) in enumerate(pairs):
        f = (lambda g, l=l, r=r, i=i: g.matmul(out_ap, lhsT=l, rhs=r, start=(i == 0), stop=(i == n - 1)))
        if i == n - 1:
            kb.op("pe", f, reads=reads, writes=[pbuf])
        else:
            kb_op_noinc(kb, "pe", f, reads=reads, writes=[pbuf])


class Dense:
    def __init__(self, kb, ident_d, tag=""):
        self.kb = kb
        nc = kb.nc
        self.ident_f = kb.sb("ident_f" + tag, [128, 128], F32)
        self.ident = kb.sb("ident_b" + tag, [128, 128], BF16)
        self.cb = Buf("consts" + tag)
        kb.dma("sp", self.ident_f[:], ident_d, writes=[self.cb], key="d_const" + tag)
        kb.op("dve", lambda g: g.tensor_copy(out=self.ident[:], in_=self.ident_f[:]), reads=[self.cb], writes=[self.cb])
        self.junk = Rot(kb, "junk" + tag, [128, 1024], BF16, 2)
        self.xs = Rot(kb, "xs" + tag, [128, 1024], BF16, 2)
        self.st = Rot(kb, "st" + tag, [128, 8], F32, 4)
        if not tag:
            self.tmp = Rot(kb, "tmpf", [128, 1024], F32, 2)

    def load_const(self, name, shape, dram_ap, dt=F32):
        t = self.kb.sb("c_" + name, shape, dt)
        self.kb.dma("sp", t[:], dram_ap, writes=[self.cb], key="d_const")
        return t

    def rstd_from_ss(self, st, sb_, ncol, denom):
        kb = self.kb
        kb.op("dve", lambda g: g.tensor_scalar(out=st[:, 0:ncol], in0=st[:, 0:ncol], scalar1=1.0 / denom, scalar2=EPS,
                                               op0=ALU.mult, op1=ALU.add), reads=[sb_], writes=[sb_])
        kb.op("act", lambda g: g.activation(out=st[:, 0:ncol], in_=st[:, 0:ncol], func=AF.Sqrt), reads=[sb_], writes=[sb_])
        kb.op("dve", lambda g: g.reciprocal(out=st[:, 4:4 + ncol], in_=st[:, 0:ncol]), reads=[sb_], writes=[sb_])

    def norm_T(self, h_ap, hbuf, gcol, uT_view, ubuf):
        kb = self.kb
        junk, jb = self.junk.get()
        st, sb_ = self.st.get()
        xs, xb = self.xs.get()
        kb.op("act", lambda g: g.activation(out=junk[:], in_=h_ap, func=AF.Square, accum_out=st[:, 0:1]),
              reads=[hbuf], writes=[jb, sb_])
        self.rstd_from_ss(st, sb_, 1, float(D))
        kb.op("dve", lambda g: g.tensor_scalar(out=xs[:], in0=h_ap, scalar1=st[:, 4:5], scalar2=None, op0=ALU.mult),
              reads=[hbuf, sb_], writes=[xb])
        pt, pb = kb.ps()
        ptb = pt[:].bitcast(BF16)
        for kc in range(8):
            f = lambda g, kc=kc: g.transpose(out=ptb[:, kc * 128:(kc + 1) * 128], in_=xs[:, kc * 128:(kc + 1) * 128],
                                             identity=self.ident[:])
            if kc == 7:
                kb.op("pe", f, reads=[xb, self.cb], writes=[pb])
            else:
                kb_op_noinc(kb, "pe", f, reads=[xb, self.cb], writes=[pb])
        kb.op("dve", lambda g: g.tensor_tensor(out=uT_view, in0=ptb.rearrange("p (k t) -> p k t", k=8),
                                               in1=gcol.to_broadcast([128, 8, 128]), op=ALU.mult),
              reads=[pb, self.cb], writes=[ubuf])

    def postnorm_res(self, halves, h_in, hinb, h_out, houtb, gbc):
        kb = self.kb
        junk, jb = self.junk.get()
        st, sb_ = self.st.get()
        for i, (pa, pb) in enumerate(halves):
            kb.op("act", lambda g, i=i, pa=pa: g.activation(out=junk[:, i * 512:(i + 1) * 512], in_=pa, func=AF.Square,
                                                           accum_out=st[:, i:i + 1]), reads=[pb], writes=[jb, sb_])
        kb.op("dve", lambda g: g.tensor_tensor(out=st[:, 0:1], in0=st[:, 0:1], in1=st[:, 1:2], op=ALU.add),
              reads=[sb_], writes=[sb_])
        self.rstd_from_ss(st, sb_, 1, float(D))
        tmp, tb = self.tmp.get()
        for i, (pa, pb) in enumerate(halves):
            kb.op("dve", lambda g, i=i, pa=pa: g.scalar_tensor_tensor(out=tmp[:, i * 512:(i + 1) * 512], in0=pa,
                                                                     scalar=st[:, 4:5], in1=gbc[:, i * 512:(i + 1) * 512],
                                                                     op0=ALU.mult, op1=ALU.mult),
                  reads=[pb, sb_, self.cb], writes=[tb])
        kb.op("pool", lambda g: g.tensor_tensor(out=h_out, in0=tmp[:], in1=h_in, op=ALU.add),
              reads=[tb, hinb], writes=[houtb])


def wview(w):
    return w.rearrange("(kc p) n -> p kc n", p=128)


class FFN:
    def __init__(self, kb, dn, NT=4):
        self.kb, self.dn, self.NT = kb, dn, NT
        self.wg = Rot(kb, "wg", [128, 8, 256], BF16, 2)
        self.wu = Rot(kb, "wu", [128, 8, 256], BF16, 2)
        self.wd = kb.sb("wd", [128, 22, 1024], BF16)
        self.wdb = [Buf("wd%d" % i) for i in range(11)]
        self.actT = kb.sb("actT", [128, 22, NT * 128], BF16)
        self.actb = [Buf("act%d" % i) for i in range(22)]
        self.sg = Rot(kb, "sg", [128, 512], F32, 2)

    def emit(self, uT, ubufs, w_gate, w_up, w_down, gbc_out, h_dram, hrot, after, side=None):
        kb, NT = self.kb, self.NT
        ntok = NT * 128
        wdv = wview(w_down)
        for j in range(11):
            kb.dma("sp", self.wd[:, 2 * j:2 * j + 2, :], wdv[:, 2 * j:2 * j + 2, :], writes=[self.wdb[j]])
        blocks = [(c, 256) for c in range(0, DFF, 256)]
        for (c0, wd_) in blocks:
            wg, wgb = self.wg.get()
            wu, wub = self.wu.get()
            if FAKE_CONTIG:
                bi = c0 // 256
                kb.dma("sp", wg[:, :, 0:wd_], w_gate.rearrange("(p a) n -> p (a n)", p=128)[:, bi * 2048:(bi + 1) * 2048].rearrange("p (k n) -> p k n", k=8), writes=[wgb])
                kb.dma("sp", wu[:, :, 0:wd_], w_up.rearrange("(p a) n -> p (a n)", p=128)[:, bi * 2048:(bi + 1) * 2048].rearrange("p (k n) -> p k n", k=8), writes=[wub])
            else:
                kb.dma("sp", wg[:, :, 0:wd_], wview(w_gate)[:, :, c0:c0 + wd_], writes=[wgb])
                kb.dma("sp", wu[:, :, 0:wd_], wview(w_up)[:, :, c0:c0 + wd_], writes=[wub])
            for oc in range(wd_ // 128):
                occ = c0 // 128 + oc
                pg, pgb = kb.ps()
                mm_group(kb, pg, pgb, pg[:, 0:ntok],
                         [(wg[:, kc, oc * 128:(oc + 1) * 128], uT[:, kc, :]) for kc in range(8)], reads=[wgb] + ubufs)
                pu, pub = kb.ps()
                mm_group(kb, pu, pub, pu[:, 0:ntok],
                         [(wu[:, kc, oc * 128:(oc + 1) * 128], uT[:, kc, :]) for kc in range(8)], reads=[wub] + ubufs)
                sg, sgb = self.sg.get()
                kb.op("act", lambda g: g.activation(out=sg[:, 0:ntok], in_=pg[:, 0:ntok], func=AF.Silu), reads=[pgb], writes=[sgb])
                kb.op("dve", lambda g: g.tensor_tensor(out=self.actT[:, occ, :], in0=sg[:, 0:ntok], in1=pu[:, 0:ntok], op=ALU.mult),
                      reads=[sgb, pub], writes=[self.actb[occ]])
        def down_stage():
            for t in range(NT):
                halves = []
                for oh in range(2):
                    py, pyb = kb.ps()
                    mm_group(kb, py, pyb, py[:, :],
                             [(self.actT[:, kc, t * 128:(t + 1) * 128], self.wd[:, kc, oh * 512:(oh + 1) * 512]) for kc in range(22)],
                             reads=self.actb + self.wdb)
                    halves.append((py[:, :], pyb))
                hi, hib = hrot.get()
                kb.dma("sp", hi[:], h_dram[t * 128:(t + 1) * 128, :], writes=[hib])
                ho, hob = hrot.get()
                self.dn.postnorm_res(halves, hi[:], hib, ho[:], hob, gbc_out)
                after(t, ho, hob)

        interleave(kb, [down_stage, side])


TPC = 2048
NPASS = TPC // 512
C_Q, C_K, C_V, C_Z, C_X, C_DT, C_G = 0, 1536, 3072, 4608, 6656, 10752, 10784


def build_A(with_win=True):
    kb = KB()
    kb.psum_banks()
    T = {"h": kb.din("h", [TPC, D], F32), "ident": kb.din("ident", [128, 128], F32), "g0col": kb.din("g0col", [128, 8], F32),
         "g2col": kb.din("g2col", [128, 8], F32), "g1bc": kb.din("g1bc", [128, D], F32), "bgcol": kb.din("bgcol", [128, 16], F32),
         "w_gate": kb.din("w_gate", [D, DFF], BF16), "w_up": kb.din("w_up", [D, DFF], BF16), "w_down": kb.din("w_down", [DFF, D], BF16),
         "w_in": kb.din("w_in", [D, INW], BF16), "h1": kb.dout("h1", [TPC, D], F32)}
    if with_win:
        T.update({"qkT": kb.dout("qkT", [3072, TPC], BF16), "v": kb.dout("v", [TPC, 1536], BF16), "z": kb.dout("z", [TPC, 2048], F32),
                  "xbcT": kb.dout("xbcT", [4096, TPC], F32), "dt": kb.dout("dt", [TPC, 32], F32), "gT": kb.dout("gT", [2048, TPC], F32)})
    emit_A(kb, T, TPC, with_win)
    return kb.finish()


def emit_A(kb, T, ntok, with_win=True):
    h, ident_d, g0col_d, g2col_d, g1bc_d, bgcol_d = T["h"], T["ident"], T["g0col"], T["g2col"], T["g1bc"], T["bgcol"]
    w_gate, w_up, w_down, w_in, h1 = T["w_gate"], T["w_up"], T["w_down"], T["w_in"], T["h1"]
    if with_win:
        qkT, v_o, z_o, xbcT, dt_o, gT = T["qkT"], T["v"], T["z"], T["xbcT"], T["dt"], T["gT"]
    toks = []
    dn = Dense(kb, ident_d)
    g0col = dn.load_const("g0col", [128, 8], g0col_d)
    g2col = dn.load_const("g2col", [128, 8], g2col_d)
    g1bc = dn.load_const("g1bc", [128, D], g1bc_d)
    bgcol = dn.load_const("bgcol", [128, 16], bgcol_d)
    kb.op("dve", lambda g: g.tensor_scalar(out=g1bc[:], in0=g1bc[:], scalar1=0.5, scalar2=None, op0=ALU.mult),
          reads=[dn.cb], writes=[dn.cb])
    ffn = FFN(kb, dn)
    hrot = Rot(kb, "hrot", [128, D], F32, 4)
    uTs = [kb.sb("uT_%d" % i, [128, 8, 512], BF16) for i in range(2)]
    ubs = [[Buf("uT%d_%d" % (i, j)) for j in range(4)] for i in range(2)]
    kb.default_banks = list(range(7))
    SIDE = ([7], "side")
    uT2 = kb.sb("uT2", [128, 8, 512], BF16)
    ub2 = [Buf("uT2%d" % i) for i in range(4)]
    win = Rot(kb, "win", [128, 8, 512], BF16, 2)
    stf = Rot(kb, "stf", [128, 512], F32, 4)
    stb = Rot(kb, "stb", [128, 512], BF16, 4)
    winv = wview(w_in)

    dn_s = Dense(kb, ident_d, tag="s")
    g0col_s = kb.sb("g0col_s", [128, 8], F32)
    kb.dma("sp", g0col_s[:], g0col_d, writes=[dn_s.cb], key="d_consts")

    def pre_norm(p):
        for t in range(4):
            hi, hib = hrot2.get()
            kb.dma("sp", hi[:], h[p * 512 + t * 128:p * 512 + (t + 1) * 128, :], writes=[hib])
            dn_s.norm_T(hi[:], hib, g0col_s[:, 0:8], uTs[p % 2][:, :, t * 128:(t + 1) * 128], ubs[p % 2][t])

    def pre_norm_side(p):
        kb.tls.pool = SIDE
        pre_norm(p)

    hrot2 = Rot(kb, "hrot2", [128, D], F32, 2)
    pre_norm(0)
    npass = ntok // 512
    for p in range(npass):
        t0 = p * 512
        uT, ub = uTs[p % 2], ubs[p % 2]

        def after(t, ho, hob, t0=t0):
            toks.append(kb.dma("pool", h1[t0 + t * 128:t0 + (t + 1) * 128, :], ho[:], reads=[hob]))
            if with_win:
                dn.norm_T(ho[:], hob, g2col[:, 0:8], uT2[:, :, t * 128:(t + 1) * 128], ub2[t])

        nxt_norm = (lambda p=p: pre_norm_side(p + 1)) if p + 1 < npass else None
        if not with_win:
            ffn.emit(uT, ub, w_gate, w_up, w_down, g1bc, h[t0:t0 + 512, :], hrot, after, side=nxt_norm)
            continue
        ffn.emit(uT, ub, w_gate, w_up, w_down, g1bc, h[t0:t0 + 512, :], hrot, after)

        def load_w(c0, wd_):
            w, wb = win.get()
            kb.dma("sp", w[:, :, 0:wd_], winv[:, :, c0:c0 + wd_], writes=[wb])
            return w, wb

        def fm_block(c0, wd_, dst, r0, dt_out, bias=None):
            w, wb = load_w(c0, wd_)
            for oc in range(wd_ // 128):
                pp, ppb = kb.ps()
                mm_group(kb, pp, ppb, pp[:, :], [(w[:, kc, oc * 128:(oc + 1) * 128], uT2[:, kc, :]) for kc in range(8)],
                         reads=[wb] + ub2)
                st, sb_ = (stb if dt_out == BF16 else stf).get()
                if bias is not None:
                    j = (r0 + oc * 128) // 128
                    kb.op("act", lambda g: g.activation(out=st[:], in_=pp[:, :], func=AF.Sigmoid, bias=bgcol[:, j:j + 1], scale=1.0),
                          reads=[ppb, dn.cb], writes=[sb_])
                elif oc % 2 == 0:
                    kb.op("act", lambda g: g.copy(out=st[:], in_=pp[:, :]), reads=[ppb], writes=[sb_])
                else:
                    kb.op("dve", lambda g: g.tensor_copy(out=st[:], in_=pp[:, :]), reads=[ppb], writes=[sb_])
                rr = r0 + oc * 128
                toks.append(kb.dma("pool", dst[rr:rr + 128, t0:t0 + 512], st[:], reads=[sb_]))

        def tm_block(c0, wd_, dst, cdst, dt_out):
            w, wb = load_w(c0, wd_)
            for t in range(4):
                pp, ppb = kb.ps()
                mm_group(kb, pp, ppb, pp[:, 0:wd_], [(uT2[:, kc, t * 128:(t + 1) * 128], w[:, kc, 0:wd_]) for kc in range(8)],
                         reads=[wb, ub2[t]])
                st, sb_ = (stb if dt_out == BF16 else stf).get()
                if t % 2 == 0:
                    kb.op("act", lambda g: g.copy(out=st[:, 0:wd_], in_=pp[:, 0:wd_]), reads=[ppb], writes=[sb_])
                else:
                    kb.op("dve", lambda g: g.tensor_copy(out=st[:, 0:wd_], in_=pp[:, 0:wd_]), reads=[ppb], writes=[sb_])
                toks.append(kb.dma("pool", dst[t0 + t * 128:t0 + (t + 1) * 128, cdst:cdst + wd_], st[:, 0:wd_], reads=[sb_]))

        def win_stage():
            for c in range(0, 1536, 512):
                tm_block(C_V + c, 512, v_o, c, BF16)
            for c in range(0, 2048, 512):
                tm_block(C_Z + c, 512, z_o, c, F32)
            tm_block(C_DT, 32, dt_o, 0, F32)
            for c in range(0, 3072, 512):
                fm_block(c, 512, qkT, c, BF16)
            for c in range(0, 4096, 512):
                fm_block(C_X + c, 512, xbcT, c, F32)
            for c in range(0, 2048, 512):
                fm_block(C_G + c, 512, gT, c, F32, bias=True)

        interleave(kb, [win_stage, nxt_norm])
    kb.default_banks = list(range(8))
    return toks


DILS = (1, 4, 16)


def build_M1(dbg=0):
    kb = KB()
    kb.psum_banks()
    S = SEQ
    qT = kb.din("qT", [768, S], BF16)
    kT = kb.din("kT", [768, S], BF16)
    v = kb.din("v", [S, 768], BF16)
    mask_d = kb.din("mask", [128, 512], F32)
    yT = kb.dout("yattT", [256, S], BF16)
    emit_M1(kb, lambda g: qT[g * 256:(g + 1) * 256, :], lambda g: kT[g * 256:(g + 1) * 256, :],
            lambda g: v[:, g * 256:(g + 1) * 256], mask_d, yT)
    return kb.finish()


def emit_M1(kb, qrows, krows, vcols, mask_d, yT, dbg=0, side=None):
    S = SEQ
    toks = []
    cb = Buf("consts")
    mask_f = kb.sb("mask_f", [128, 512], F32)
    mask = kb.sb("mask_b", [128, 512], BF16)
    ones_f = kb.sb("ones_f", [128, 64], F32)
    kb.dma("sp", mask_f[:], mask_d, writes=[cb], key="d_const")
    kb.op("dve", lambda g: g.tensor_copy(out=mask[:], in_=mask_f[:]), reads=[cb], writes=[cb])
    kb.op("dve", lambda g: g.memset(ones_f[:], 0.0), writes=[cb])
    kb.op("dve", lambda g: g.memset(ones_f[64:65, :], 1.0), writes=[cb])
    qs = kb.sb("qs", [128, 2, S], BF16)
    ks = kb.sb("ks", [128, 2, S], BF16)
    qb, kbuf = Buf("qs"), Buf("ks")
    vst = kb.sb("vst", [128, 32, 256], BF16)
    vstb = Buf("vst")
    vaug = kb.sb("vaug", [128, 32, 4, 65], BF16)
    vab = Buf("vaug")
    tot = kb.sb("tot", [128, 4, S], F32)
    totb = Buf("tot")
    prot = Rot(kb, "P", [128, 512], BF16, 6)
    yst = Rot(kb, "yst", [64, 512], BF16, 3)
    PS_S = ([0, 1, 2, 3], "s")
    PS_O = ([4, 5, 6, 7], "o")
    side = list(side) if side else []
    nseg = 96
    per_seg = (len(side) + nseg - 1) // nseg if side else 0
    if side:
        CH = max(w_ for (_, _, w_) in side)
        cin = Rot(kb, "cin", [128, CH], F32, 3)
        cout = Rot(kb, "cout", [128, CH], BF16, 3)
    seg_ctr = [0]

    def side_fn():
        k0 = seg_ctr[0] * per_seg
        seg_ctr[0] += 1
        items = side[k0:k0 + per_seg]
        if not items:
            return None

        def run_():
            for (src, dst, w_) in items:
                ti, bi = cin.get()
                to, bo = cout.get()
                kb.dma("sp", ti[:, 0:w_], src, writes=[bi])
                kb.op("pool", lambda g_: g_.tensor_copy(out=to[:, 0:w_], in_=ti[:, 0:w_]), reads=[bi], writes=[bo])
                kb.dma("pool", dst, to[:, 0:w_], reads=[bo])
        return run_
    kb.op("pool", lambda g: g.memset(vaug[:], 1.0), writes=[vab])

    for g in range(3):
        d = DILS[g]
        nb = S // (128 * d)
        kb.dma("sp", qs[:], qrows(g).rearrange("(c p) t -> p c t", p=128), writes=[qb])
        kb.dma("sp", ks[:], krows(g).rearrange("(c p) t -> p c t", p=128), writes=[kbuf])
        vv = vcols(g).rearrange("(n i r) f -> r i n f", i=128, r=d)
        pairs = []
        for r in range(d):
            for n0 in range(0, nb, 8):
                n1 = min(nb, n0 + 8)
                pairs.append((vst[:, r * nb + n0:r * nb + n1, :], vv[r][:, n0:n1, :]))
        kb.dma_multi("sp", pairs, writes=[vstb])
        kb.op("pool", lambda g_: g_.tensor_copy(out=vaug[:, :, :, 0:64], in_=vst[:].rearrange("p b (h e) -> p b h e", h=4)),
              reads=[vstb], writes=[vab])

        def tk(r, n):
            s0 = r + d * 128 * n
            return slice(s0, s0 + d * 127 + 1, d)

        if dbg == 1 and g == 0:
            kb.op("dve", lambda g_: g_.memset(tot[:], 1.0), writes=[totb])
        def make_block(r, n):
            blk = r * nb + n
            pss = [kb.ps(PS_S), kb.ps(PS_S)]
            Ps = [prot.get(), prot.get()]
            po, pob = kb.ps(PS_O)

            def E():
                for ch in range(2):
                    for hp in range(2):
                        pst, psb = pss[hp]
                        pl = slice(hp * 64, (hp + 1) * 64)
                        if n > 0:
                            kb_op_noinc(kb, "pe", lambda g_: g_.matmul(pst[:, ch * 256:ch * 256 + 128], lhsT=ks[pl, ch, tk(r, n - 1)],
                                                                      rhs=qs[pl, ch, tk(r, n)], start=True, stop=True),
                                        reads=[qb, kbuf], writes=[psb])
                        f = lambda g_: g_.matmul(pst[:, ch * 256 + 128:ch * 256 + 256], lhsT=ks[pl, ch, tk(r, n)],
                                                 rhs=qs[pl, ch, tk(r, n)], start=True, stop=True)
                        if ch == 1:
                            kb.op("pe", f, reads=[qb, kbuf], writes=[psb])
                        else:
                            kb_op_noinc(kb, "pe", f, reads=[qb, kbuf], writes=[psb])
                for hp in range(2):
                    pst, psb = pss[hp]
                    P, Pb = Ps[hp]
                    if n > 0:
                        sv, pv, mv = pst[:, :], P[:], mask[:]
                    else:
                        sel = lambda a: a.rearrange("p (h c) -> p h c", h=2)[:, :, 128:256]
                        sv, pv, mv = sel(pst[:, :]), sel(P[:]), sel(mask[:])
                    kb.op("act", lambda g_: g_.activation(out=pv, in_=sv, func=AF.Exp, scale=0.125), reads=[psb], writes=[Pb])
                    kb.op("dve", lambda g_: g_.tensor_tensor(out=pv, in0=pv, in1=mv, op=ALU.mult), reads=[Pb, cb], writes=[Pb])

            def Lt():
                for hp in range(2):
                    P, Pb = Ps[hp]
                    for ch in range(2):
                        hd = 2 * ch + hp
                        oap = po[0:65, hd * 128:(hd + 1) * 128]
                        if n > 0:
                            kb_op_noinc(kb, "pe", lambda g_: g_.matmul(oap, lhsT=vaug[:, blk - 1, hd, :], rhs=P[:, ch * 256:ch * 256 + 128],
                                                                      start=True, stop=False), reads=[Pb, vab], writes=[pob])
                        f = lambda g_: g_.matmul(oap, lhsT=vaug[:, blk, hd, :], rhs=P[:, ch * 256 + 128:ch * 256 + 256],
                                                 start=(n == 0), stop=True)
                        if hp == 1 and ch == 1:
                            kb.op("pe", f, reads=[Pb, vab], writes=[pob])
                        else:
                            kb_op_noinc(kb, "pe", f, reads=[Pb, vab], writes=[pob])
                tv = tot[0:65, :, tk(r, n)]
                pov = po[0:65, :].rearrange("p (h q) -> p h q", h=4)
                if g == 0:
                    kb.op("dve", lambda g_: g_.tensor_copy(out=tv, in_=pov), reads=[pob], writes=[totb])
                else:
                    kb.op("dve", lambda g_: g_.tensor_tensor(out=tv, in0=tv, in1=pov, op=ALU.add), reads=[pob, totb], writes=[totb])

            return E, Lt

        prevLt = None
        for r in range(d):
            for n in range(nb):
                E, Lt = make_block(r, n)
                interleave(kb, [E, prevLt, side_fn()])
                prevLt = Lt
        interleave(kb, [prevLt])
    den = tot[64:65, :, :]
    for hd in range(4):
        for c0 in range(0, S, 2048):
            dv = tot[64:65, hd, c0:c0 + 2048]
            kb.op("act", lambda g_: g_.activation(out=dv, in_=dv, func=AF.Ln), reads=[totb], writes=[totb])
            kb.op("act", lambda g_: g_.activation(out=dv, in_=dv, func=AF.Exp, scale=-1.0), reads=[totb], writes=[totb])
    for hd in range(4):
        for tb in range(S // 512):
            pb_, pbb = kb.ps(PS_S)
            ts_ = slice(tb * 512, (tb + 1) * 512)
            kb.op("pe", lambda g_: g_.matmul(pb_[0:64, :], lhsT=ones_f[0:65, 0:64], rhs=tot[0:65, hd, ts_], start=True, stop=True),
                  reads=[totb, cb], writes=[pbb])
            ys, ysb = yst.get()
            kb.op("dve", lambda g_: g_.tensor_tensor(out=ys[:], in0=tot[0:64, hd, ts_], in1=pb_[0:64, :], op=ALU.mult),
                  reads=[totb, pbb], writes=[ysb])
            toks.append(kb.dma("pool", yT[hd * 64:(hd + 1) * 64, ts_], ys[:], reads=[ysb]))
    return toks


def attn_mask():
    j = np.arange(128)[:, None]
    i = np.arange(128)[None, :]
    prev = (j >= i).astype(np.float32)
    cur = (j <= i).astype(np.float32)
    return np.ascontiguousarray(np.concatenate([prev, cur, prev, cur], axis=1))


def build_M2():
    kb = KB()
    kb.psum_banks()
    S = SEQ
    xbcT = kb.din("xbcT", [2048, S], F32)
    T = {"xrow": lambda c: xbcT[c * 128:(c + 1) * 128, :], "z": kb.din("z", [S, 1024], F32), "dt": kb.din("dt", [S, 16], F32),
         "cw": kb.din("cw", [128, 64], F32), "cbias": kb.din("cbias", [128, 16], F32), "dtb": kb.din("dtb", [128, 16], F32),
         "alog": kb.din("alog", [128, 16], F32), "dsk": kb.din("dsk", [128, 16], F32), "ng": kb.din("ng", [128, 1024], F32),
         "U": kb.din("U", [128, 128], F32), "ident": kb.din("ident", [128, 128], F32), "yT": kb.dout("yssdT", [1024, S], BF16)}
    emit_M2(kb, T)
    return kb.finish()


def emit_M2(kb, T, side=None):
    S = SEQ
    xrow, z_d, dt_d, cw_d, cbias_d, dtb_d, alog_d, dsk_d, ng_d, U_d, ident_d, yT_o = (
        T["xrow"], T["z"], T["dt"], T["cw"], T["cbias"], T["dtb"], T["alog"], T["dsk"], T["ng"], T["U"], T["ident"], T["yT"])
    toks = []
    cb = Buf("consts")

    def cload(name, shape, ap):
        t = kb.sb("c_" + name, shape, F32)
        kb.dma("sp", t[:], ap, writes=[cb], key="d_const")
        return t

    cw = cload("cw", [128, 64], cw_d)
    cbias = cload("cbias", [128, 16], cbias_d)
    dtb = cload("dtb", [128, 16], dtb_d)
    Aneg = cload("alog", [128, 16], alog_d)
    dsk = cload("dsk", [128, 16], dsk_d)
    ng = cload("ng", [128, 1024], ng_d)
    U = cload("U", [128, 128], U_d)
    ident_f = cload("ident", [128, 128], ident_d)
    ident = kb.sb("ident_b", [128, 128], BF16)
    ones = kb.sb("ones_f", [128, 128], F32)
    kb.op("dve", lambda g: g.tensor_copy(out=ident[:], in_=ident_f[:]), reads=[cb], writes=[cb])
    kb.op("dve", lambda g: g.memset(ones[:], 1.0), writes=[cb])
    negf = kb.sb("negf", [128, 128], F32)
    neg = kb.sb("neg_b", [128, 512], BF16)
    kb.op("dve", lambda g: g.tensor_scalar(out=negf[:], in0=U[:], scalar1=30000.0, scalar2=-30000.0, op0=ALU.mult, op1=ALU.add),
          reads=[cb], writes=[cb])
    kb.op("dve", lambda g: g.tensor_copy(out=neg[:].rearrange("p (q l) -> p q l", q=4), in_=negf[:].unsqueeze(1).to_broadcast([128, 4, 128])),
          reads=[cb], writes=[cb])
    kb.op("act", lambda g: g.activation(out=Aneg[:], in_=Aneg[:], func=AF.Exp), reads=[cb], writes=[cb])
    kb.op("dve", lambda g: g.tensor_scalar(out=Aneg[:], in0=Aneg[:], scalar1=-1.0, scalar2=None, op0=ALU.mult), reads=[cb], writes=[cb])

    xin = Rot(kb, "xin", [128, 515], F32, 4)
    acc = Rot(kb, "acc", [128, 512], F32, 2)
    xc_r = Rot(kb, "xc", [128, 16, 512], BF16, 2)
    xtm_r = Rot(kb, "xtm", [128, 1024], BF16, 2)
    btm_r = Rot(kb, "btm", [128, 512], BF16, 2)
    xdt_r = Rot(kb, "xdt", [128, 1024], BF16, 2)
    xdd_r = Rot(kb, "xdd", [128, 1024], BF16, 2)
    z_r = Rot(kb, "zc", [128, 4, 1024], F32, 2)
    cbm_r = Rot(kb, "cbm", [128, 512], F32, 2)
    L_r = Rot(kb, "Lm", [128, 512], F32, 2)
    yb_r = Rot(kb, "yb", [128, 1024], F32, 2)
    t4_r = Rot(kb, "t4", [128, 1024], F32, 2)
    yn_r = Rot(kb, "yn", [128, 1024], BF16, 2)
    junk_r = Rot(kb, "junk", [128, 256], BF16, 2)
    yT_r = Rot(kb, "yTs", [128, 8, 512], BF16, 2)
    dtr_r = Rot(kb, "dtr", [128, 4, 16], F32, 2)
    dtw_r = Rot(kb, "dtw", [128, 6, 64], F32, 2)
    sm_r = Rot(kb, "sm", [128, 8, 16], F32, 3)
    prev = kb.sb("prev", [128, 1024], F32)
    prev_bf = kb.sb("prev_bf", [128, 1024], BF16)
    ptmp = kb.sb("ptmp", [128, 1024], F32)
    pvb, pbb, ptb_ = Buf("prev"), Buf("prev_bf"), Buf("ptmp")
    kb.op("dve", lambda g: g.memset(prev[:], 0.0), writes=[pvb])
    kb.op("dve", lambda g: g.memset(prev_bf[:], 0.0), writes=[pbb])

    B_XT, B_BT, B_CB, B_R, B_PY, B_PO, B_PS, B_CV = range(8)
    B_YT = B_XT
    xcbuf_sets = [[Buf("xc%d_%d" % (i, c)) for c in range(16)] for i in range(2)]
    Wall_r = Rot(kb, "Wall", [128, 16 * 128], BF16, 2)
    Wbuf_sets = [[Buf("W%d_%d" % (i, g)) for g in range(4)] for i in range(2)]

    def bank(i):
        return kb.pbank[i]

    h3 = lambda ap: ap.rearrange("p (h e) -> p h e", e=64)
    bch = lambda ap, lo, n: ap[:, lo:lo + n].to_broadcast([128, n, 64])
    q3 = lambda ap: ap.rearrange("p (q l) -> p q l", q=4)
    noinc = lambda *a_, **k_: kb_op_noinc(kb, *a_, **k_)

    def prologue(tb):
        t0 = tb * 512
        xc, xcb = xc_r.get()
        xcbufs = xcbuf_sets[tb % 2]
        for c in range(16):
            xi, xib = xin.get()
            if tb == 0:
                kb.op("pool", lambda g: g.memset(xi[:, 0:3], 0.0), writes=[xib])
                kb.dma("sp", xi[:, 3:515], xrow(c)[:, 0:512], writes=[xib])
            else:
                kb.dma("sp", xi[:], xrow(c)[:, t0 - 3:t0 + 512], writes=[xib])
            ac, acb = acc.get()
            kb.op("dve", lambda g: g.tensor_scalar(out=ac[:], in0=xi[:, 3:515], scalar1=cw[:, c * 4 + 3:c * 4 + 4],
                                                   scalar2=cbias[:, c:c + 1], op0=ALU.mult, op1=ALU.add),
                  reads=[xib, cb], writes=[acb])
            for k in (2, 1, 0):
                kb.op("dve", lambda g, k=k: g.scalar_tensor_tensor(out=ac[:], in0=xi[:, k:k + 512], scalar=cw[:, c * 4 + k:c * 4 + k + 1],
                                                                   in1=ac[:], op0=ALU.mult, op1=ALU.add),
                      reads=[xib, cb, acb], writes=[acb])
            kb.op("act", lambda g: g.activation(out=xc[:, c, :], in_=ac[:], func=AF.Silu), reads=[acb], writes=[xcbufs[c]])
        zc, zcb = z_r.get()
        kb.dma_multi("sp", [(zc[:, j, :], z_d[t0 + j * 128:t0 + (j + 1) * 128, :]) for j in range(4)], writes=[zcb])
        for j in range(4):
            kb.op("act", lambda g, j=j: g.activation(out=zc[:, j, :], in_=zc[:, j, :], func=AF.Silu), reads=[zcb], writes=[zcb])
        dtr, dtrb = dtr_r.get()
        kb.dma("sp", dtr[:], dt_d[t0:t0 + 512, :].rearrange("(j p) h -> p j h", p=128), writes=[dtrb])
        dw, dwb = dtw_r.get()
        T_, AX_, E_, L_, DT_, A_ = [dw[:, i, :] for i in range(6)]
        v3 = lambda ap: ap.rearrange("p (j h) -> p j h", j=4)
        kb.op("dve", lambda g: g.tensor_tensor(out=v3(T_), in0=dtr[:], in1=dtb[:].unsqueeze(1).to_broadcast([128, 4, 16]), op=ALU.add),
              reads=[dtrb, cb], writes=[dwb])
        kb.op("act", lambda g: g.activation(out=AX_, in_=T_, func=AF.Abs), reads=[dwb], writes=[dwb])
        kb.op("act", lambda g: g.activation(out=E_, in_=AX_, func=AF.Exp, scale=-1.0), reads=[dwb], writes=[dwb])
        kb.op("act", lambda g: g.activation(out=L_, in_=E_, func=AF.Ln, bias=1.0, scale=1.0), reads=[dwb], writes=[dwb])
        kb.op("dve", lambda g: g.scalar_tensor_tensor(out=DT_, in0=T_, scalar=0.0, in1=L_, op0=ALU.max, op1=ALU.add),
              reads=[dwb], writes=[dwb])
        kb.op("dve", lambda g: g.tensor_tensor(out=v3(A_), in0=v3(DT_), in1=Aneg[:].unsqueeze(1).to_broadcast([128, 4, 16]), op=ALU.mult),
              reads=[dwb, cb], writes=[dwb])
        yTs, yTb = yT_r.get()
        return dict(xc=xc, xall=xcbufs, zc=zc, zcb=zcb, DT_=DT_, A_=A_, dwb=dwb, yTs=yTs, yTb=yTb, t0=t0)

    def make_chunk(B, j, J):
        xc, xall, zc, zcb, dwb, yTs, yTb = B["xc"], B["xall"], B["zc"], B["zcb"], B["dwb"], B["yTs"], B["yTb"]
        js = slice(j * 128, (j + 1) * 128)
        hs = slice(j * 16, (j + 1) * 16)
        dt_j = B["DT_"][:, hs]
        a_j = B["A_"][:, hs]
        xtm, xtmb = xtm_r.get()
        btm, btmb = btm_r.get()
        sm, smb = sm_r.get()
        xdt, xdtb = xdt_r.get()
        xdd, xddb = xdd_r.get()
        Wall, _ = Wall_r.get()
        Wbs = Wbuf_sets[J % 2]
        NACS, DOUT, DSTS, WST, CD, TMP, SS, RS = [sm[:, i, :] for i in range(8)]

        def E():
            pxt, pxtb = bank(B_XT)
            pxv = pxt[:].bitcast(BF16)
            with kb.atomic():
                for c in range(8):
                    f = lambda g, c=c: g.transpose(out=pxv[:, c * 128:(c + 1) * 128], in_=xc[:, c, js], identity=ident[:])
                    (kb.op if c == 7 else noinc)("pe", f, reads=[xall[c], cb], writes=[pxtb])
                kb.op("act", lambda g: g.copy(out=xtm[:], in_=pxv), reads=[pxtb], writes=[xtmb])
            if kb.hook is not None:
                kb.hook()
            pbt, pbtb = bank(B_BT)
            pbv = pbt[:].bitcast(BF16)
            for c in range(4):
                f = lambda g, c=c: g.transpose(out=pbv[:, c * 128:(c + 1) * 128], in_=xc[:, 8 + c, js], identity=ident[:])
                noinc("pe", f, reads=[xall[8 + c], cb], writes=[pbtb])
            noinc("pe", lambda g: g.matmul(pbt[:, 256:272], lhsT=U[:], rhs=a_j, start=True, stop=True), reads=[dwb, cb], writes=[pbtb])
            kb.op("pe", lambda g: g.matmul(pbt[:, 272:288], lhsT=ones[:], rhs=a_j, start=True, stop=True), reads=[dwb, cb], writes=[pbtb])
            kb.op("dve", lambda g: g.tensor_copy(out=btm[:], in_=pbv[:, 0:512]), reads=[pbtb], writes=[btmb])
            acs_p, tot_p = pbt[:, 256:272], pbt[:, 272:288]
            kb.op("dve", lambda g: g.tensor_scalar(out=NACS, in0=acs_p, scalar1=-1.0, scalar2=None, op0=ALU.mult), reads=[pbtb], writes=[smb])
            kb.op("act", lambda g: g.activation(out=DOUT, in_=acs_p, func=AF.Exp), reads=[pbtb], writes=[smb])
            kb.op("dve", lambda g: g.tensor_tensor(out=TMP, in0=tot_p, in1=NACS, op=ALU.add), reads=[pbtb, smb], writes=[smb])
            kb.op("act", lambda g: g.activation(out=DSTS, in_=TMP, func=AF.Exp), reads=[smb], writes=[smb])
            kb.op("act", lambda g: g.activation(out=CD, in_=tot_p, func=AF.Exp), reads=[pbtb], writes=[smb])
            kb.op("dve", lambda g: g.tensor_tensor(out=h3(xdt[:]), in0=h3(xtm[:]), in1=bch(dt_j, 0, 16), op=ALU.mult),
                  reads=[xtmb, dwb], writes=[xdtb])
            kb.op("dve", lambda g: g.tensor_tensor(out=h3(xdd[:]), in0=h3(xdt[:]), in1=bch(DSTS, 0, 16), op=ALU.mult),
                  reads=[xdtb, smb], writes=[xddb])
            pcb, pcbb = bank(B_CB)
            for g4 in range(4):
                f = lambda g, g4=g4: g.matmul(pcb[:, g4 * 128:(g4 + 1) * 128], lhsT=xc[:, 8 + g4, js], rhs=xc[:, 12 + g4, js], start=True, stop=True)
                (kb.op if g4 == 3 else noinc)("pe", f, reads=[xall[8 + g4], xall[12 + g4]], writes=[pcbb])
            cbm, cbmb = cbm_r.get()
            kb.op("dve", lambda g: g.tensor_tensor(out=cbm[:].rearrange("p (g l) -> p g l", g=4), in0=pcb[:, :].rearrange("p (g l) -> p g l", g=4),
                                                   in1=U[:].unsqueeze(1).to_broadcast([128, 4, 128]), op=ALU.mult),
                  reads=[pcbb, cb], writes=[cbmb])
            for g4 in range(4):
                pR, pRb = bank(B_R if g4 % 2 == 0 else B_CV)
                noinc("pe", lambda g: g.matmul(pR[:, :], lhsT=ident[:], rhs=neg[:], start=True, stop=False), reads=[cb], writes=[pRb])
                for q in range(4):
                    h = 4 * g4 + q
                    f = lambda g, q=q, h=h: g.matmul(pR[:, q * 128:(q + 1) * 128], lhsT=a_j[:, h:h + 1].to_broadcast([128, 128]), rhs=U[:],
                                                     start=False, stop=(q == 3))
                    (kb.op if q == 3 else noinc)("pe", f, reads=[dwb, cb], writes=[pRb])
                Lm, Lb = L_r.get()
                kb.op("dve", lambda g: g.tensor_tensor(out=q3(Lm[:]), in0=q3(pR[:, :]), in1=NACS[:, 4 * g4:4 * g4 + 4].to_broadcast([128, 4, 128]),
                                                       op=ALU.add), reads=[pRb, smb], writes=[Lb])
                kb.op("act", lambda g: g.activation(out=Lm[:], in_=Lm[:], func=AF.Exp), reads=[Lb], writes=[Lb])
                kb.op("dve", lambda g: g.tensor_tensor(out=q3(Wall[:, g4 * 512:(g4 + 1) * 512]), in0=q3(Lm[:]),
                                                       in1=cbm[:, g4 * 128:(g4 + 1) * 128].unsqueeze(1).to_broadcast([128, 4, 128]),
                                                       op=ALU.mult), reads=[Lb, cbmb], writes=[Wbs[g4]])

        def Lt():
            yb, ybb = yb_r.get()
            for hh in range(2):
                py, pyb = bank(B_PY)
                po, pob = bank(B_PO)
                pst, pstb = bank(B_PS)
                for gq in range(2):
                    g4 = 2 * hh + gq
                    for q in range(4):
                        h = 4 * g4 + q
                        col = (h - 8 * hh) * 64
                        f = lambda g, h=h, col=col: g.matmul(py[:, col:col + 64], lhsT=Wall[:, h * 128:(h + 1) * 128], rhs=xdt[:, h * 64:(h + 1) * 64],
                                                             start=True, stop=True)
                        (kb.op if (q == 3 and gq == 1) else noinc)("pe", f, reads=[Wbs[g4], xdtb], writes=[pyb])
                for gq in range(2):
                    g4 = 2 * hh + gq
                    f = lambda g, gq=gq, g4=g4: g.matmul(po[:, gq * 256:(gq + 1) * 256], lhsT=xc[:, 12 + g4, js], rhs=prev_bf[:, g4 * 256:(g4 + 1) * 256],
                                                         start=True, stop=True)
                    (kb.op if gq == 1 else noinc)("pe", f, reads=[xall[12 + g4], pbb], writes=[pob])
                for gq in range(2):
                    g4 = 2 * hh + gq
                    f = lambda g, gq=gq, g4=g4: g.matmul(pst[:, gq * 256:(gq + 1) * 256], lhsT=btm[:, g4 * 128:(g4 + 1) * 128], rhs=xdd[:, g4 * 256:(g4 + 1) * 256],
                                                         start=True, stop=True)
                    (kb.op if gq == 1 else noinc)("pe", f, reads=[btmb, xddb], writes=[pstb])
                cs = slice(hh * 512, (hh + 1) * 512)
                kb.op("dve", lambda g: g.tensor_tensor(out=h3(yb[:, cs]), in0=h3(po[:, :]), in1=bch(DOUT, 8 * hh, 8), op=ALU.mult),
                      reads=[pob, smb], writes=[ybb])
                kb.op("dve", lambda g: g.tensor_tensor(out=yb[:, cs], in0=yb[:, cs], in1=py[:, :], op=ALU.add), reads=[pyb, ybb], writes=[ybb])
                kb.op("pool", lambda g: g.tensor_tensor(out=h3(ptmp[:, cs]), in0=h3(prev[:, cs]), in1=bch(CD, 8 * hh, 8), op=ALU.mult),
                      reads=[pvb, smb], writes=[ptb_])
                kb.op("dve", lambda g: g.tensor_tensor(out=prev[:, cs], in0=ptmp[:, cs], in1=pst[:, :], op=ALU.add), reads=[ptb_, pstb], writes=[pvb])
                kb.op("act", lambda g: g.copy(out=prev_bf[:, cs], in_=prev[:, cs]), reads=[pvb], writes=[pbb])
            t4, t4b = t4_r.get()
            kb.op("pool", lambda g: g.tensor_tensor(out=h3(t4[:]), in0=h3(xtm[:]), in1=bch(dsk, 0, 16), op=ALU.mult), reads=[xtmb, cb], writes=[t4b])
            kb.op("pool", lambda g: g.tensor_tensor(out=yb[:], in0=yb[:], in1=t4[:], op=ALU.add), reads=[t4b, ybb], writes=[ybb])
            kb.op("dve", lambda g: g.tensor_tensor(out=yb[:], in0=yb[:], in1=zc[:, j, :], op=ALU.mult), reads=[zcb, ybb], writes=[ybb])
            for g4 in range(4):
                jk, jkb = junk_r.get()
                kb.op("act", lambda g, g4=g4: g.activation(out=jk[:], in_=yb[:, g4 * 256:(g4 + 1) * 256], func=AF.Square, accum_out=SS[:, g4:g4 + 1]),
                      reads=[ybb], writes=[jkb, smb])
            kb.op("dve", lambda g: g.tensor_scalar(out=SS[:, 0:4], in0=SS[:, 0:4], scalar1=1.0 / 256.0, scalar2=EPS, op0=ALU.mult, op1=ALU.add),
                  reads=[smb], writes=[smb])
            kb.op("act", lambda g: g.activation(out=SS[:, 0:4], in_=SS[:, 0:4], func=AF.Ln), reads=[smb], writes=[smb])
            kb.op("act", lambda g: g.activation(out=RS[:, 0:4], in_=SS[:, 0:4], func=AF.Exp, scale=-0.5), reads=[smb], writes=[smb])
            kb.op("dve", lambda g: g.tensor_tensor(out=yb[:].rearrange("p (g e) -> p g e", g=4), in0=yb[:].rearrange("p (g e) -> p g e", g=4),
                                                   in1=RS[:, 0:4].to_broadcast([128, 4, 256]), op=ALU.mult), reads=[ybb, smb], writes=[ybb])
            yn, ynb = yn_r.get()
            kb.op("pool", lambda g: g.tensor_tensor(out=yn[:], in0=yb[:], in1=ng[:], op=ALU.mult), reads=[ybb, cb], writes=[ynb])
            pyt, pytb = bank(B_YT)
            pyv = pyt[:].bitcast(BF16)
            with kb.atomic():
                for c in range(8):
                    f = lambda g, c=c: g.transpose(out=pyv[:, c * 128:(c + 1) * 128], in_=yn[:, c * 128:(c + 1) * 128], identity=ident[:])
                    (kb.op if c == 7 else noinc)("pe", f, reads=[ynb, cb], writes=[pytb])
                kb.op("act", lambda g: g.copy(out=yTs[:, :, js], in_=pyv.rearrange("p (c t) -> p c t", c=8)), reads=[pytb], writes=[yTb])
            if kb.hook is not None:
                kb.hook()
            if j == 3:
                toks.append(kb.dma("pool", yT_o.rearrange("(c p) t -> p c t", p=128)[:, :, B["t0"]:B["t0"] + 512], yTs[:], reads=[yTb]))

        return E, Lt

    prevLt = None
    J = 0
    nblk = S // 512
    sc = SideCast(kb, side or [], 32)
    B = prologue(0)
    for tb in range(nblk):
        nxt = {}
        for j in range(4):
            E, Lt = make_chunk(B, j, J)
            pro = (lambda: nxt.update(B=prologue(tb + 1))) if (j == 1 and tb + 1 < nblk) else None
            interleave(kb, [E, prevLt, pro, sc.next()])
            prevLt = Lt
            J += 1
        B = nxt.get("B")
    interleave(kb, [prevLt, sc.next(last=True)])
    return toks


def m2_host_inputs(hf, xbcT_full, z_full, dt_full, conv_w, conv_b, dt_bias, a_log, d_skip, ssd_norm_g):
    ch = np.concatenate([np.arange(hf * 1024, (hf + 1) * 1024), 2048 + np.arange(hf * 512, (hf + 1) * 512),
                         3072 + np.arange(hf * 512, (hf + 1) * 512)])
    bc = lambda v_: np.ascontiguousarray(np.broadcast_to(np.asarray(v_, np.float32)[None, :], (128, len(v_))))
    cw = np.asarray(conv_w)[:, ch].reshape(4, 16, 128).transpose(2, 1, 0).reshape(128, 64)
    hs = slice(hf * 16, (hf + 1) * 16)
    full = xbcT_full.shape[0] == 4096 and xbcT_full.shape[1] > 4
    return {
        "xbcT": np.ascontiguousarray(xbcT_full[ch]) if full else None,
        "z": np.ascontiguousarray(z_full[:, hf * 1024:(hf + 1) * 1024]) if full else None,
        "dt": np.ascontiguousarray(dt_full[:, hs]) if full else None,
        "cw": np.ascontiguousarray(cw, dtype=np.float32),
        "cbias": np.ascontiguousarray(np.asarray(conv_b)[ch].reshape(16, 128).T, dtype=np.float32),
        "dtb": bc(np.asarray(dt_bias)[hs]), "alog": bc(np.asarray(a_log)[hs]), "dsk": bc(np.asarray(d_skip)[hs]),
        "ng": bc(np.asarray(ssd_norm_g)[hf * 1024:(hf + 1) * 1024]),
        "U": np.triu(np.ones((128, 128), np.float32)), "ident": np.eye(128, dtype=np.float32),
    }


def build_B1():
    kb = KB()
    kb.psum_banks()
    T = {"h1": kb.din("h1", [TPC, D], F32), "yattT": kb.din("yattT", [512, TPC], BF16), "yssdT": kb.din("yssdT", [2048, TPC], BF16),
         "gT": kb.din("gT", [2048, TPC], F32), "mem": kb.din("mem", [NMEM, D], F32), "ident": kb.din("ident", [128, 128], F32),
         "g4col": kb.din("g4col", [128, 8], F32), "mgcol": kb.din("mgcol", [128, 8], F32), "g3bc": kb.din("g3bc", [128, D], F32),
         "g5bc": kb.din("g5bc", [128, D], F32), "w_att": kb.din("w_att", [512, D], BF16), "w_ssd": kb.din("w_ssd", [2048, D], BF16),
         "w_o": kb.din("w_o", [D, D], BF16), "wq": kb.din("wq", [D, D], BF16), "wk": kb.din("wk", [D, D], BF16),
         "wv": kb.din("wv", [D, D], BF16), "wo": kb.din("wo", [D, D], BF16), "h3": kb.dout("h3", [TPC, D], F32)}
    emit_B1(kb, T, TPC)
    return kb.finish()


def emit_B1(kb, T, ntok):
    h1, yattT, yssdT, gT, mem, ident_d = T["h1"], T["yattT"], T["yssdT"], T["gT"], T["mem"], T["ident"]
    g4col_d, mgcol_d, g3bc_d, g5bc_d = T["g4col"], T["mgcol"], T["g3bc"], T["g5bc"]
    w_att, w_ssd, w_o, wq, wk, wv, wo, h3 = T["w_att"], T["w_ssd"], T["w_o"], T["wq"], T["wk"], T["wv"], T["wo"], T["h3"]
    toks = []
    dn = Dense(kb, ident_d)
    g4col = dn.load_const("g4col", [128, 8], g4col_d)
    mgcol = dn.load_const("mgcol", [128, 8], mgcol_d)
    g3bc = dn.load_const("g3bc", [128, D], g3bc_d)
    g5bc = dn.load_const("g5bc", [128, D], g5bc_d)
    ones_b = kb.sb("ones_b", [128, 128], BF16)
    kb.op("dve", lambda g: g.memset(ones_b[:], 1.0), writes=[dn.cb])
    wrot = Rot(kb, "wr", [128, 8, 1024], BF16, 3)
    hrot = Rot(kb, "hrot", [128, D], F32, 4)
    h2t = [(kb.sb("h2_%d" % i, [128, D], F32), Buf("h2_%d" % i)) for i in range(4)]
    ya = kb.sb("ya", [128, 4, 512], BF16)
    ys = kb.sb("ys", [128, 16, 512], BF16)
    yab, ysb = Buf("ya"), Buf("ys")
    grot = Rot(kb, "gr", [128, 512], F32, 4)
    m12 = Rot(kb, "m12", [128, 512], F32, 4)
    mT = kb.sb("mT", [128, 8, 512], BF16)
    mTb = [Buf("mT%d" % i) for i in range(8)]
    uT = kb.sb("uT", [128, 8, 512], BF16)
    ub = [Buf("uT%d" % i) for i in range(4)]
    memT = kb.sb("memT", [128, 8, 256], BF16)
    memTb = [Buf("memT0"), Buf("memT1")]
    KmT = kb.sb("KmT", [128, 8, 256], BF16)
    Vm = kb.sb("Vm", [128, 2, 1024], BF16)
    kmb, vmb = Buf("KmT"), Buf("Vm")
    QT = kb.sb("QT", [128, 8, 512], BF16)
    QTb = [Buf("QT%d" % i) for i in range(8)]
    OT = kb.sb("OT", [128, 8, 512], BF16)
    OTb = [Buf("OT%d" % i) for i in range(8)]
    PTr = Rot(kb, "PT", [128, 512], BF16, 4)
    rdr = Rot(kb, "rd", [128, 512], F32, 2)

    def loadw(w, kcn=8):
        t, b = wrot.get()
        kb.dma("sp", t[:, 0:kcn, :], wview(w)[:, 0:kcn, :], writes=[b])
        return t, b

    for mt in range(2):
        hi, hib = hrot.get()
        kb.dma("sp", hi[:], mem[mt * 128:(mt + 1) * 128, :], writes=[hib])
        dn.norm_T(hi[:], hib, mgcol[:, 0:8], memT[:, :, mt * 128:(mt + 1) * 128], memTb[mt])
    wkt, wkb = loadw(wk)
    for oc in range(8):
        pp, ppb = kb.ps()
        mm_group(kb, pp, ppb, pp[:, 0:256], [(wkt[:, kc, oc * 128:(oc + 1) * 128], memT[:, kc, :]) for kc in range(8)], reads=[wkb] + memTb)
        kb.op("act", lambda g: g.copy(out=KmT[:, oc, :], in_=pp[:, 0:256]), reads=[ppb], writes=[kmb])
    wvt, wvb = loadw(wv)
    for mt in range(2):
        for oh in range(2):
            pp, ppb = kb.ps()
            mm_group(kb, pp, ppb, pp[:, :], [(memT[:, kc, mt * 128:(mt + 1) * 128], wvt[:, kc, oh * 512:(oh + 1) * 512]) for kc in range(8)],
                     reads=[wvb] + memTb)
            kb.op("dve", lambda g: g.tensor_copy(out=Vm[:, mt, oh * 512:(oh + 1) * 512], in_=pp[:, :]), reads=[ppb], writes=[vmb])

    for p in range(ntok // 512):
        t0 = p * 512
        ts_ = slice(t0, t0 + 512)
        kb.dma("sp", ya[:], yattT.rearrange("(c p) t -> p c t", p=128)[:, :, ts_], writes=[yab])
        kb.dma("sp", ys[:, 0:8, :], yssdT.rearrange("(c p) t -> p c t", p=128)[:, 0:8, ts_], writes=[ysb])
        kb.dma("sp", ys[:, 8:16, :], yssdT.rearrange("(c p) t -> p c t", p=128)[:, 8:16, ts_], writes=[ysb])
        wa, wab = loadw(w_att, 4)
        ws0, ws0b = loadw(w_ssd[0:1024, :])
        ws1, ws1b = loadw(w_ssd[1024:2048, :])
        for oc in range(8):
            ocs = slice(oc * 128, (oc + 1) * 128)
            pa, pab = kb.ps()
            mm_group(kb, pa, pab, pa[:, :], [(wa[:, kc, ocs], ya[:, kc, :]) for kc in range(4)], reads=[wab, yab])
            pb_, pbb = kb.ps()
            mm_group(kb, pb_, pbb, pb_[:, :], [((ws0 if kc < 8 else ws1)[:, kc % 8, ocs], ys[:, kc, :]) for kc in range(16)],
                     reads=[ws0b, ws1b, ysb])
            g0, g0b = grot.get()
            g1, g1b = grot.get()
            kb.dma("sp", g0[:], gT[oc * 128:(oc + 1) * 128, ts_], writes=[g0b])
            kb.dma("sp", g1[:], gT[1024 + oc * 128:1024 + (oc + 1) * 128, ts_], writes=[g1b])
            ma, mab = m12.get()
            mb_, mbb = m12.get()
            kb.op("dve", lambda g: g.tensor_tensor(out=ma[:], in0=g0[:], in1=pa[:, :], op=ALU.mult), reads=[g0b, pab], writes=[mab])
            kb.op("dve", lambda g: g.tensor_tensor(out=mb_[:], in0=g1[:], in1=pb_[:, :], op=ALU.mult), reads=[g1b, pbb], writes=[mbb])
            kb.op("pool", lambda g: g.tensor_tensor(out=mT[:, oc, :], in0=ma[:], in1=mb_[:], op=ALU.add), reads=[mab, mbb], writes=[mTb[oc]])
        wot, wob = loadw(w_o)
        for t in range(4):
            halves = []
            for oh in range(2):
                py, pyb = kb.ps()
                mm_group(kb, py, pyb, py[:, :], [(mT[:, kc, t * 128:(t + 1) * 128], wot[:, kc, oh * 512:(oh + 1) * 512]) for kc in range(8)],
                         reads=mTb + [wob])
                halves.append((py[:, :], pyb))
            hi, hib = hrot.get()
            kb.dma("sp", hi[:], h1[t0 + t * 128:t0 + (t + 1) * 128, :], writes=[hib])
            dn.postnorm_res(halves, hi[:], hib, h2t[t][0][:], h2t[t][1], g3bc)
            dn.norm_T(h2t[t][0][:], h2t[t][1], g4col[:, 0:8], uT[:, :, t * 128:(t + 1) * 128], ub[t])
        wqt, wqb = loadw(wq)
        for oc in range(8):
            pp, ppb = kb.ps()
            mm_group(kb, pp, ppb, pp[:, :], [(wqt[:, kc, oc * 128:(oc + 1) * 128], uT[:, kc, :]) for kc in range(8)], reads=[wqb] + ub)
            if oc % 2 == 0:
                kb.op("act", lambda g: g.copy(out=QT[:, oc, :], in_=pp[:, :]), reads=[ppb], writes=[QTb[oc]])
            else:
                kb.op("dve", lambda g: g.tensor_copy(out=QT[:, oc, :], in_=pp[:, :]), reads=[ppb], writes=[QTb[oc]])
        for hx in range(4):
            PTs = []
            for mc in range(2):
                pS, pSb = kb.ps()
                mm_group(kb, pS, pSb, pS[:, :], [(KmT[:, 2 * hx + dc, mc * 128:(mc + 1) * 128], QT[:, 2 * hx + dc, :]) for dc in range(2)],
                         reads=[kmb, QTb[2 * hx], QTb[2 * hx + 1]])
                PT, PTb = PTr.get()
                kb.op("act", lambda g: g.activation(out=PT[:], in_=pS[:, :], func=AF.Exp, scale=1.0 / 16.0), reads=[pSb], writes=[PTb])
                PTs.append((PT, PTb))
            pD, pDb = kb.ps()
            mm_group(kb, pD, pDb, pD[:, :], [(ones_b[:], PTs[mc][0][:]) for mc in range(2)], reads=[dn.cb, PTs[0][1], PTs[1][1]])
            rd, rdb = rdr.get()
            kb.op("dve", lambda g: g.reciprocal(out=rd[:], in_=pD[:, :]), reads=[pDb], writes=[rdb])
            for dc in range(2):
                oc = 2 * hx + dc
                pO, pOb = kb.ps()
                mm_group(kb, pO, pOb, pO[:, :], [(Vm[:, mc, oc * 128:(oc + 1) * 128], PTs[mc][0][:]) for mc in range(2)],
                         reads=[vmb, PTs[0][1], PTs[1][1]])
                kb.op("dve", lambda g: g.tensor_tensor(out=OT[:, oc, :], in0=rd[:], in1=pO[:, :], op=ALU.mult), reads=[rdb, pOb], writes=[OTb[oc]])
        wxo, wxob = loadw(wo)
        for t in range(4):
            halves = []
            for oh in range(2):
                py, pyb = kb.ps()
                mm_group(kb, py, pyb, py[:, :], [(OT[:, kc, t * 128:(t + 1) * 128], wxo[:, kc, oh * 512:(oh + 1) * 512]) for kc in range(8)],
                         reads=OTb + [wxob])
                halves.append((py[:, :], pyb))
            ho, hob = hrot.get()
            dn.postnorm_res(halves, h2t[t][0][:], h2t[t][1], ho[:], hob, g5bc)
            toks.append(kb.dma("pool", h3[t0 + t * 128:t0 + (t + 1) * 128, :], ho[:], reads=[hob]))
    return toks


def _col(v_):
    return np.ascontiguousarray(np.asarray(v_, np.float32).reshape(-1, 128).T)


def _bc(v_):
    v_ = np.asarray(v_, np.float32)
    return np.ascontiguousarray(np.broadcast_to(v_[None, :], (128, v_.size)))


def _prog(name, fn):
    if name not in _cache:
        _cache[name] = fn()
    return _cache[name]


def kernel_unfused(x, mem, norm_g, ffn1_gate, ffn1_up, ffn1_down, w_in, b_gate, conv_w, conv_b,
           dt_bias, a_log, d_skip, ssd_norm_g, w_att_out, w_ssd_out, w_o, mem_norm_g,
           xa_wq, xa_wk, xa_wv, xa_wo, ffn2_gate, ffn2_up, ffn2_down):
    f32 = lambda a: np.asarray(a, np.float32)
    x, mem, norm_g = f32(x), f32(mem), f32(norm_g)
    ident = np.eye(128, dtype=np.float32)
    mask = attn_mask()
    h = np.ascontiguousarray(x.reshape(NCORES, TPC, D))
    hs = [h[c] for c in range(NCORES)]
    big = {"ffn1_gate": ffn1_gate, "ffn1_up": ffn1_up, "ffn1_down": ffn1_down, "w_in": w_in, "w_att_out": w_att_out,
           "w_ssd_out": w_ssd_out, "w_o": w_o, "xa_wq": xa_wq, "xa_wk": xa_wk, "xa_wv": xa_wv, "xa_wo": xa_wo,
           "ffn2_gate": ffn2_gate, "ffn2_up": ffn2_up, "ffn2_down": ffn2_down}
    for L in range(DEPTH):
        wb = cast_weights({k: f32(v_[L]) for k, v_ in big.items()})
        g = norm_g[L]
        ncA = _prog("A", lambda: build_A(True))
        resA = run(ncA, [{"h": hs[c], "ident": ident, "g0col": _col(g[0]), "g2col": _col(g[2]), "g1bc": _bc(g[1]),
                          "bgcol": _col(f32(b_gate[L])), "w_gate": wb["ffn1_gate"], "w_up": wb["ffn1_up"],
                          "w_down": wb["ffn1_down"], "w_in": wb["w_in"]} for c in range(NCORES)])
        cat_t = lambda name, b, ax: np.concatenate([np.asarray(resA[2 * b][name]), np.asarray(resA[2 * b + 1][name])], axis=ax)
        insM1, insM2 = [], []
        for b in range(BATCH):
            qk = cat_t("qkT", b, 1)
            vf = cat_t("v", b, 0)
            zf = cat_t("z", b, 0)
            xf = cat_t("xbcT", b, 1)
            df = cat_t("dt", b, 0)
            for hf in range(2):
                rows = np.concatenate([np.arange(gi * 512 + hf * 256, gi * 512 + hf * 256 + 256) for gi in range(3)])
                insM1.append({"qT": np.ascontiguousarray(qk[rows]), "kT": np.ascontiguousarray(qk[1536 + rows]),
                              "v": np.ascontiguousarray(vf[:, rows]), "mask": mask})
                insM2.append(m2_host_inputs(hf, xf, zf, df, f32(conv_w[L]), f32(conv_b[L]), f32(dt_bias[L]), f32(a_log[L]),
                                            f32(d_skip[L]), f32(ssd_norm_g[L])))
        resM1 = run(_prog("M1", build_M1), insM1)
        resM2 = run(_prog("M2", build_M2), insM2)
        insB = []
        for c in range(NCORES):
            b, half = c // 2, c % 2
            tsl = slice(half * TPC, (half + 1) * TPC)
            ya = np.concatenate([np.asarray(resM1[2 * b][("yattT")]), np.asarray(resM1[2 * b + 1]["yattT"])], axis=0)[:, tsl]
            ys = np.concatenate([np.asarray(resM2[2 * b]["yssdT"]), np.asarray(resM2[2 * b + 1]["yssdT"])], axis=0)[:, tsl]
            insB.append({"h1": np.asarray(resA[c]["h1"]), "yattT": np.ascontiguousarray(ya), "yssdT": np.ascontiguousarray(ys),
                         "gT": np.asarray(resA[c]["gT"]), "mem": np.ascontiguousarray(mem[b]), "ident": ident,
                         "g4col": _col(g[4]), "mgcol": _col(f32(mem_norm_g[L])), "g3bc": _bc(g[3]), "g5bc": _bc(g[5]),
                         "w_att": wb["w_att_out"], "w_ssd": wb["w_ssd_out"], "w_o": wb["w_o"], "wq": wb["xa_wq"],
                         "wk": wb["xa_wk"], "wv": wb["xa_wv"], "wo": wb["xa_wo"]})
        resB = run(_prog("B1", build_B1), insB)
        resF = run(_prog("B2", lambda: build_A(False)),
                   [{"h": np.asarray(resB[c]["h3"]), "ident": ident, "g0col": _col(g[6]), "g2col": _col(g[6]), "g1bc": _bc(g[7]),
                     "bgcol": _col(f32(b_gate[L])), "w_gate": wb["ffn2_gate"], "w_up": wb["ffn2_up"], "w_down": wb["ffn2_down"],
                     "w_in": wb["w_in"]} for c in range(NCORES)])
        hs = [np.asarray(resF[c]["h1"]) for c in range(NCORES)]
    out = np.stack(hs, axis=0).reshape(BATCH, SEQ, D).astype(np.float32)
    return out


BIGW = {"ffn1_gate": (D, DFF), "ffn1_up": (D, DFF), "ffn1_down": (DFF, D), "w_in": (D, INW), "w_att_out": (512, D),
        "w_ssd_out": (2048, D), "w_o": (D, D), "xa_wq": (D, D), "xa_wk": (D, D), "xa_wv": (D, D), "xa_wo": (D, D),
        "ffn2_gate": (D, DFF), "ffn2_up": (D, DFF), "ffn2_down": (DFF, D)}


def build_fused(depth=DEPTH):
    kb = KB()
    kb.psum_banks()
    S = SEQ
    x = kb.din("x", [S, D], F32)
    mem = kb.din("mem", [NMEM, D], F32)
    ident = kb.din("ident", [128, 128], F32)
    mask = kb.din("mask", [128, 512], F32)
    U = kb.din("U", [128, 128], F32)
    gcol = kb.din("gcol", [DEPTH * 8, 128, 8], F32)
    gbc = kb.din("gbc", [DEPTH * 8, 128, D], F32)
    bgcol = kb.din("bgcol", [DEPTH, 128, 16], F32)
    mgcol = kb.din("mgcol", [DEPTH, 128, 8], F32)
    cw = kb.din("cw", [DEPTH * 2, 128, 64], F32)
    cbias = kb.din("cbias", [DEPTH * 2, 128, 16], F32)
    dtb = kb.din("dtb", [DEPTH * 2, 128, 16], F32)
    alog = kb.din("alog", [DEPTH * 2, 128, 16], F32)
    dsk = kb.din("dsk", [DEPTH * 2, 128, 16], F32)
    ng = kb.din("ng", [DEPTH * 2, 128, 1024], F32)
    wf = {n: kb.din(n, [DEPTH, k, m], F32) for n, (k, m) in BIGW.items()}
    wb = {n: kb.dint(n + "_bf", [DEPTH, k, m], BF16) for n, (k, m) in BIGW.items()}
    out = kb.dout("out", [S, D], F32)
    qkT = kb.dint("s_qkT", [3072, S], BF16)
    v = kb.dint("s_v", [S, 1536], BF16)
    z = kb.dint("s_z", [S, 2048], F32)
    xbcT = kb.dint("s_xbcT", [4096, S], F32)
    dt = kb.dint("s_dt", [S, 32], F32)
    gT = kb.dint("s_gT", [2048, S], F32)
    yattT = kb.dint("s_yattT", [512, S], BF16)
    yssdT = kb.dint("s_yssdT", [2048, S], BF16)
    hA = kb.dint("s_hA", [S, D], F32)
    hB = kb.dint("s_hB", [S, D], F32)
    hC = kb.dint("s_hC", [S, D], F32)

    def flat(ap):
        return ap.rearrange("k n -> (k n)").rearrange("(p f) -> p f", p=128)

    for n, (k, m) in BIGW.items():
        for L0 in range(depth):
            emit_cast(kb, flat(wf[n][L0]), flat(wb[n][L0]), (k * m) // 128)
            kb.end_phase()
            kb.begin_phase()

    def cast_pieces(L, CH=1536):
        out_ = []
        for n, (k, m) in BIGW.items():
            N = (k * m) // 128
            sv, dv = flat(wf[n][L]), flat(wb[n][L])
            for lo in range(0, N, CH):
                w_ = min(CH, N - lo)
                out_.append((sv[:, lo:lo + w_], dv[:, lo:lo + w_], w_))
        return out_

    hin = x
    for L in range(depth):
        TA = {"h": hin, "ident": ident, "g0col": gcol[L * 8 + 0], "g2col": gcol[L * 8 + 2], "g1bc": gbc[L * 8 + 1], "bgcol": bgcol[L],
              "w_gate": wb["ffn1_gate"][L], "w_up": wb["ffn1_up"][L], "w_down": wb["ffn1_down"][L], "w_in": wb["w_in"][L], "h1": hA,
              "qkT": qkT, "v": v, "z": z, "xbcT": xbcT, "dt": dt, "gT": gT}
        emit_A(kb, TA, S, True)
        kb.end_phase()
        pieces = []
        half_ = (len(pieces) + 1) // 2
        for hf in range(2):
            kb.begin_phase()
            emit_M1(kb, lambda g, hf=hf: qkT[g * 512 + hf * 256:g * 512 + hf * 256 + 256, :],
                    lambda g, hf=hf: qkT[1536 + g * 512 + hf * 256:1536 + g * 512 + hf * 256 + 256, :],
                    lambda g, hf=hf: v[:, g * 512 + hf * 256:g * 512 + hf * 256 + 256], mask, yattT[hf * 256:(hf + 1) * 256, :],
                    side=None)
            kb.end_phase()
        for hf in range(2):
            kb.begin_phase()

            def xrow(c, hf=hf):
                if c < 8:
                    r0 = hf * 1024 + c * 128
                elif c < 12:
                    r0 = 2048 + hf * 512 + (c - 8) * 128
                else:
                    r0 = 3072 + hf * 512 + (c - 12) * 128
                return xbcT[r0:r0 + 128, :]

            i2 = L * 2 + hf
            TM = {"xrow": xrow, "z": z[:, hf * 1024:(hf + 1) * 1024], "dt": dt[:, hf * 16:(hf + 1) * 16], "cw": cw[i2], "cbias": cbias[i2],
                  "dtb": dtb[i2], "alog": alog[i2], "dsk": dsk[i2], "ng": ng[i2], "U": U, "ident": ident,
                  "yT": yssdT[hf * 1024:(hf + 1) * 1024, :]}
            emit_M2(kb, TM, side=pieces[hf * half_:(hf + 1) * half_])
            kb.end_phase()
        kb.begin_phase()
        TB = {"h1": hA, "yattT": yattT, "yssdT": yssdT, "gT": gT, "mem": mem, "ident": ident, "g4col": gcol[L * 8 + 4], "mgcol": mgcol[L],
              "g3bc": gbc[L * 8 + 3], "g5bc": gbc[L * 8 + 5], "w_att": wb["w_att_out"][L], "w_ssd": wb["w_ssd_out"][L], "w_o": wb["w_o"][L],
              "wq": wb["xa_wq"][L], "wk": wb["xa_wk"][L], "wv": wb["xa_wv"][L], "wo": wb["xa_wo"][L], "h3": hB}
        emit_B1(kb, TB, S)
        kb.end_phase()
        kb.begin_phase()
        hout = out if L == depth - 1 else hC
        TF = {"h": hB, "ident": ident, "g0col": gcol[L * 8 + 6], "g2col": gcol[L * 8 + 6], "g1bc": gbc[L * 8 + 7], "bgcol": bgcol[L],
              "w_gate": wb["ffn2_gate"][L], "w_up": wb["ffn2_up"][L], "w_down": wb["ffn2_down"][L], "w_in": wb["w_in"][L], "h1": hout}
        emit_A(kb, TF, S, False)
        if L < depth - 1:
            kb.end_phase()
            kb.begin_phase()
        hin = hC
    return kb.finish()


def fused_inputs(x_b, mem_b, P):
    f32 = lambda a: np.ascontiguousarray(np.asarray(a, np.float32))
    m = {"x": f32(x_b), "mem": f32(mem_b), "ident": np.eye(128, dtype=np.float32), "mask": attn_mask(),
         "U": np.triu(np.ones((128, 128), np.float32))}
    ngm = np.asarray(P["norm_g"], np.float32)
    m["gcol"] = np.ascontiguousarray(ngm.reshape(DEPTH * 8, 8, 128).transpose(0, 2, 1))
    m["gbc"] = np.ascontiguousarray(np.broadcast_to(ngm.reshape(DEPTH * 8, 1, D), (DEPTH * 8, 128, D)))
    m["bgcol"] = np.ascontiguousarray(np.asarray(P["b_gate"], np.float32).reshape(DEPTH, 16, 128).transpose(0, 2, 1))
    m["mgcol"] = np.ascontiguousarray(np.asarray(P["mem_norm_g"], np.float32).reshape(DEPTH, 8, 128).transpose(0, 2, 1))
    cw, cbias, dtb, alog, dsk, ng = [], [], [], [], [], []
    dummy = np.zeros((4096, 4), np.float32)
    for L in range(DEPTH):
        for hf in range(2):
            r = m2_host_inputs(hf, dummy, dummy[:, :0].reshape(4096, 0) if False else np.zeros((4, 2048), np.float32),
                               np.zeros((4, 32), np.float32), P["conv_w"][L], P["conv_b"][L], P["dt_bias"][L], P["a_log"][L],
                               P["d_skip"][L], P["ssd_norm_g"][L])
            cw.append(r["cw"]); cbias.append(r["cbias"]); dtb.append(r["dtb"]); alog.append(r["alog"]); dsk.append(r["dsk"]); ng.append(r["ng"])
    for k_, lst in (("cw", cw), ("cbias", cbias), ("dtb", dtb), ("alog", alog), ("dsk", dsk), ("ng", ng)):
        m[k_] = np.ascontiguousarray(np.stack(lst, axis=0), dtype=np.float32)
    for n in BIGW:
        m[n] = f32(P[n])
    return m


def kernel(x, mem, norm_g, ffn1_gate, ffn1_up, ffn1_down, w_in, b_gate, conv_w, conv_b,
           dt_bias, a_log, d_skip, ssd_norm_g, w_att_out, w_ssd_out, w_o, mem_norm_g,
           xa_wq, xa_wk, xa_wv, xa_wo, ffn2_gate, ffn2_up, ffn2_down):
    P = dict(norm_g=norm_g, ffn1_gate=ffn1_gate, ffn1_up=ffn1_up, ffn1_down=ffn1_down, w_in=w_in, b_gate=b_gate, conv_w=conv_w,
             conv_b=conv_b, dt_bias=dt_bias, a_log=a_log, d_skip=d_skip, ssd_norm_g=ssd_norm_g, w_att_out=w_att_out,
             w_ssd_out=w_ssd_out, w_o=w_o, mem_norm_g=mem_norm_g, xa_wq=xa_wq, xa_wk=xa_wk, xa_wv=xa_wv, xa_wo=xa_wo,
             ffn2_gate=ffn2_gate, ffn2_up=ffn2_up, ffn2_down=ffn2_down)
    P = {k_: np.asarray(v_, np.float32) for k_, v_ in P.items()}
    x = np.asarray(x, np.float32)
    mem = np.asarray(mem, np.float32)
    nc = _prog("fused", build_fused)
    base = fused_inputs(x[0], mem[0], P)
    zero = {k_: np.zeros_like(v_) for k_, v_ in base.items()}
    in_maps = [zero] * NCORES
    for b in range(BATCH):
        mb_ = dict(base)
        mb_["x"] = np.ascontiguousarray(x[b])
        mb_["mem"] = np.ascontiguousarray(mem[b])
        in_maps[WORK_CORES[b]] = mb_
    res = run_bass_kernel_spmd(nc, in_maps, core_ids=list(range(NCORES)))
    return np.stack([np.asarray(res.results[WORK_CORES[b]]["out"], np.float32) for b in range(BATCH)], axis=0)
```

```python
from contextlib import ExitStack
import numpy as np
import ml_dtypes
import concourse.bass as bass
import concourse.mybir as mybir
from concourse.bass_utils import run_bass_kernel_spmd

F32 = mybir.dt.float32
BF16 = mybir.dt.bfloat16
AF = mybir.ActivationFunctionType
ALU = mybir.AluOpType
AX = mybir.AxisListType
NPBF = ml_dtypes.bfloat16

NCORES = 8
D = 1024
DFF = 2816
DEPTH = 4
BATCH = 4
SEQ = 4096
NMEM = 256
INW = 12832
EPS = 1e-6
FAKE_CONTIG = False
HQ = "act"
WORK_CORES = (0, 1, 4, 5)


class Buf:
    __slots__ = ("name", "w", "r")

    def __init__(self, name):
        self.name = name
        self.w = None
        self.r = []


class KB:
    def __init__(self):
        self.nc = bass.Bass("TRN2", target_bir_lowering=False)
        self.es = ExitStack()
        nc = self.nc
        self.eng = {"pe": nc.tensor, "act": nc.scalar, "dve": nc.vector, "pool": nc.gpsimd, "sp": nc.sync}
        self.sem = {}
        self.cnt = {}
        for e in self.eng:
            self.sem[e] = self.es.enter_context(nc.semaphore("s_" + e))
            self.cnt[e] = 0
        self.known = {e: {} for e in self.eng}
        self.sem_pool = []
        self.nps = 0
        self.pcnt = {}
        import threading
        self.tls = threading.local()
        self.default_banks = list(range(8))
        self.phase_id = 0
        self.hook = None
        self.pes = None
        self.ndram = 0
        self.begin_phase()

    def begin_phase(self):
        self.phase_id += 1
        self.pes = ExitStack()
        self.keymap = {}
        self.next_pool = 0

    def end_phase(self):
        self.barrier()
        self.pes.close()
        self.pes = None

    def sb(self, name, shape, dt):
        return self.pes.enter_context(self.nc.sbuf_tensor("p%d_%s" % (self.phase_id, name), list(shape), dt))

    def psum_banks(self):
        self.pbank = []
        for i in range(8):
            t = self.es.enter_context(self.nc.psum_tensor("psb%d" % i, [128, 512], F32))
            self.pbank.append((t, Buf("psb%d" % i)))

    def ps(self, pool=None):
        if pool is None:
            pool = getattr(self.tls, "pool", None)
        if pool is None:
            t, b = self.pbank[self.default_banks[self.nps % len(self.default_banks)]]
            self.nps += 1
            return t, b
        ids, key = pool
        n = self.pcnt.get(key, 0)
        self.pcnt[key] = n + 1
        return self.pbank[ids[n % len(ids)]]

    def barrier(self):
        for e in self.eng:
            for f in self.eng:
                if f != e and self.cnt[f] > self.known[e].get(f, 0):
                    self.eng[e].wait_ge(self.sem[f], self.cnt[f])
                    self.known[e][f] = self.cnt[f]
            for k, (s_, v) in enumerate(self.sem_pool):
                if v > self.known[e].get(k, 0):
                    self.eng[e].wait_ge(s_, v)
                    self.known[e][k] = v

    def din(self, name, shape, dt):
        return self.nc.dram_tensor(name, list(shape), dt, kind="ExternalInput").ap()

    def dout(self, name, shape, dt):
        return self.nc.dram_tensor(name, list(shape), dt, kind="ExternalOutput").ap()

    def dint(self, name, shape, dt):
        return self.nc.dram_tensor(name, list(shape), dt, kind="Internal").ap()

    def _sem_for(self, key):
        if key in self.eng:
            return self.sem[key]
        return self.sem_pool[key][0]

    def _waits(self, e, reads, writes):
        deps = {}
        for b in reads:
            if b.w is not None:
                k, v = b.w
                deps[k] = max(deps.get(k, 0), v)
        for b in writes:
            if b.w is not None:
                k, v = b.w
                deps[k] = max(deps.get(k, 0), v)
            for (k, v) in b.r:
                deps[k] = max(deps.get(k, 0), v)
        engine = self.eng[e]
        for k, v in deps.items():
            if k == e and e == "pe":
                continue
            if self.known[e].get(k, 0) >= v:
                continue
            engine.wait_ge(self._sem_for(k), v)
            self.known[e][k] = v

    def op(self, e, fn, reads=(), writes=()):
        self._waits(e, reads, writes)
        ins = fn(self.eng[e])
        self.cnt[e] += 1
        ins.then_inc(self.sem[e], 1)
        tok = (e, self.cnt[e])
        for b in reads:
            b.r.append(tok)
        for b in writes:
            b.w = tok
            b.r = []
        if self.hook is not None:
            self.hook()
        return ins

    def atomic(self):
        kb = self

        class _A:
            def __enter__(self_):
                self_.h = kb.hook
                kb.hook = None

            def __exit__(self_, *a):
                kb.hook = self_.h
        return _A()

    def _pool_idx(self, key):
        if key not in self.keymap:
            if self.next_pool >= len(self.sem_pool):
                s_ = self.es.enter_context(self.nc.semaphore("dq%d" % len(self.sem_pool)))
                self.sem_pool.append([s_, 0])
            self.keymap[key] = self.next_pool
            self.next_pool += 1
        return self.keymap[key]

    def dma(self, q, out, in_, reads=(), writes=(), key=None, **kw):
        return self.dma_multi(q, [(out, in_)], reads=reads, writes=writes, key=key, **kw)

    def dma_multi(self, q, pairs, reads=(), writes=(), key=None, **kw):
        self._waits(q, reads, writes)
        if key is None:
            key = "d_" + (writes[0].name if writes else reads[0].name)
        idx = self._pool_idx(key)
        ent = self.sem_pool[idx]
        for (out, in_) in pairs:
            ent[1] += 16
            self.eng[q].dma_start(out=out, in_=in_, **kw).then_inc(ent[0], 16)
        tok = (idx, ent[1])
        for b in reads:
            b.r.append(tok)
        for b in writes:
            b.w = tok
            b.r = []
        return tok

    def finish(self, out_toks=()):
        self.end_phase()
        self.es.close()
        return self.nc


def emit_cast(kb, x, y, N, CH=4096):
    nch = (N + CH - 1) // CH
    NB = 3
    xin = [(kb.sb("xin%d" % i, [128, CH], F32), Buf("xin%d" % i)) for i in range(NB)]
    xo = [(kb.sb("xo%d" % i, [128, CH], BF16), Buf("xo%d" % i)) for i in range(NB)]
    engs = ["dve", "act", "pool"]
    for c in range(nch):
        lo = c * CH
        w = min(CH, N - lo)
        ti, bi = xin[c % NB]
        to, bo = xo[c % NB]
        kb.dma("sp", ti[:, 0:w], x[:, lo:lo + w], writes=[bi])
        e = engs[c % 3]
        if e == "act":
            kb.op("act", lambda g: g.copy(out=to[:, 0:w], in_=ti[:, 0:w]), reads=[bi], writes=[bo])
        else:
            kb.op(e, lambda g: g.tensor_copy(out=to[:, 0:w], in_=ti[:, 0:w]), reads=[bi], writes=[bo])
        kb.dma("pool", y[:, lo:lo + w], to[:, 0:w], reads=[bo])


def build_cast(N):
    kb = KB()
    x = kb.din("x", [128, N], F32)
    y = kb.dout("y", [128, N], BF16)
    emit_cast(kb, x, y, N)
    return kb.finish()


_cache = {}


def run(nc, in_maps):
    res = run_bass_kernel_spmd(nc, in_maps, core_ids=list(range(NCORES)))
    return res.results


def cast_weights(arrs):
    flat = [np.ascontiguousarray(a, dtype=np.float32).reshape(-1) for a in arrs.values()]
    tot = sum(f.size for f in flat)
    per = NCORES * 128
    N = (tot + per - 1) // per
    N = (N + 63) // 64 * 64
    big = np.zeros(per * N, np.float32)
    big[:tot] = np.concatenate(flat)
    big = big.reshape(NCORES, 128, N)
    key = ("cast", N)
    if key not in _cache:
        _cache[key] = build_cast(N)
    res = run(_cache[key], [{"x": big[i]} for i in range(NCORES)])
    out = np.concatenate([np.asarray(r["y"]).reshape(-1) for r in res])
    outs = {}
    off = 0
    for name, a in arrs.items():
        outs[name] = out[off:off + a.size].reshape(a.shape)
        off += a.size
    return outs


def interleave(kb, fns):
    import threading
    fns = [f for f in fns if f is not None]
    if len(fns) == 1:
        fns[0]()
        return
    n = len(fns)
    sems = [threading.Semaphore(0) for _ in fns]
    done = [False] * n
    cur = [0]
    errs = []
    main = threading.Semaphore(0)

    def nxt(i):
        for k in range(1, n + 1):
            j = (i + k) % n
            if not done[j]:
                return j
        return None

    def hook():
        i = cur[0]
        j = nxt(i)
        if j is None or j == i:
            return
        cur[0] = j
        sems[j].release()
        sems[i].acquire()

    def runner(i):
        sems[i].acquire()
        try:
            fns[i]()
        except BaseException as e:
            errs.append(e)
        done[i] = True
        j = nxt(i)
        if j is not None:
            cur[0] = j
            sems[j].release()
        else:
            main.release()

    old = kb.hook
    kb.hook = hook
    ths = [threading.Thread(target=runner, args=(i,)) for i in range(n)]
    for t in ths:
        t.start()
    cur[0] = 0
    sems[0].release()
    main.acquire()
    for t in ths:
        t.join()
    kb.hook = old
    if errs:
        raise errs[0]


class SideCast:
    def __init__(self, kb, pieces, nseg):
        self.kb, self.pieces, self.k = kb, list(pieces), 0
        self.per = (len(self.pieces) + nseg - 1) // nseg if self.pieces else 0
        if self.pieces:
            CH = max(w_ for (_, _, w_) in self.pieces)
            self.cin = Rot(kb, "cin", [128, CH], F32, 3)
            self.cout = Rot(kb, "cout", [128, CH], BF16, 3)

    def next(self, last=False):
        items = self.pieces[self.k:] if last else self.pieces[self.k:self.k + self.per]
        self.k += len(items)
        if not items:
            return None
        kb = self.kb

        def run_():
            for (src, dst, w_) in items:
                ti, bi = self.cin.get()
                to, bo = self.cout.get()
                kb.dma("sp", ti[:, 0:w_], src, writes=[bi])
                kb.op("pool", lambda g_: g_.tensor_copy(out=to[:, 0:w_], in_=ti[:, 0:w_]), reads=[bi], writes=[bo])
                kb.dma("pool", dst, to[:, 0:w_], reads=[bo])
        return run_


class Rot:
    def __init__(self, kb, name, shape, dt, n):
        self.items = [(kb.sb("%s%d" % (name, i), shape, dt), Buf("%s%d" % (name, i))) for i in range(n)]
        self.i = 0

    def get(self):
        it = self.items[self.i % len(self.items)]
        self.i += 1
        return it


def kb_op_noinc(kb, e, fn, reads=(), writes=()):
    kb._waits(e, reads, writes)
    fn(kb.eng[e])
    tok = (e, kb.cnt[e] + 1)
    for b in reads:
        b.r.append(tok)
    for b in writes:
        b.w = tok
        b.r = []


def mm_group(kb, pst, pbuf, out_ap, pairs, reads):
    n = len(pairs)
    for i, (l, r) in enumerate(pairs):
        f = (lambda g, l=l, r=r, i=i: g.matmul(out_ap, lhsT=l, rhs=r, start=(i == 0), stop=(i == n - 1)))
        if i == n - 1:
            kb.op("pe", f, reads=reads, writes=[pbuf])
        else:
            kb_op_noinc(kb, "pe", f, reads=reads, writes=[pbuf])


class Dense:
    def __init__(self, kb, ident_d, tag=""):
        self.kb = kb
        nc = kb.nc
        self.ident_f = kb.sb("ident_f" + tag, [128, 128], F32)
        self.ident = kb.sb("ident_b" + tag, [128, 128], BF16)
        self.cb = Buf("consts" + tag)
        kb.dma("sp", self.ident_f[:], ident_d, writes=[self.cb], key="d_const" + tag)
        kb.op("dve", lambda g: g.tensor_copy(out=self.ident[:], in_=self.ident_f[:]), reads=[self.cb], writes=[self.cb])
        self.junk = Rot(kb, "junk" + tag, [128, 1024], BF16, 2)
        self.xs = Rot(kb, "xs" + tag, [128, 1024], BF16, 2)
        self.st = Rot(kb, "st" + tag, [128, 8], F32, 4)
        if not tag:
            self.tmp = Rot(kb, "tmpf", [128, 1024], F32, 2)

    def load_const(self, name, shape, dram_ap, dt=F32):
        t = self.kb.sb("c_" + name, shape, dt)
        self.kb.dma("sp", t[:], dram_ap, writes=[self.cb], key="d_const")
        return t

    def rstd_from_ss(self, st, sb_, ncol, denom):
        kb = self.kb
        kb.op("dve", lambda g: g.tensor_scalar(out=st[:, 0:ncol], in0=st[:, 0:ncol], scalar1=1.0 / denom, scalar2=EPS,
                                               op0=ALU.mult, op1=ALU.add), reads=[sb_], writes=[sb_])
        kb.op("act", lambda g: g.activation(out=st[:, 0:ncol], in_=st[:, 0:ncol], func=AF.Sqrt), reads=[sb_], writes=[sb_])
        kb.op("dve", lambda g: g.reciprocal(out=st[:, 4:4 + ncol], in_=st[:, 0:ncol]), reads=[sb_], writes=[sb_])

    def norm_T(self, h_ap, hbuf, gcol, uT_view, ubuf):
        kb = self.kb
        junk, jb = self.junk.get()
        st, sb_ = self.st.get()
        xs, xb = self.xs.get()
        kb.op("act", lambda g: g.activation(out=junk[:], in_=h_ap, func=AF.Square, accum_out=st[:, 0:1]),
              reads=[hbuf], writes=[jb, sb_])
        self.rstd_from_ss(st, sb_, 1, float(D))
        kb.op("dve", lambda g: g.tensor_scalar(out=xs[:], in0=h_ap, scalar1=st[:, 4:5], scalar2=None, op0=ALU.mult),
              reads=[hbuf, sb_], writes=[xb])
        pt, pb = kb.ps()
        ptb = pt[:].bitcast(BF16)
        for kc in range(8):
            f = lambda g, kc=kc: g.transpose(out=ptb[:, kc * 128:(kc + 1) * 128], in_=xs[:, kc * 128:(kc + 1) * 128],
                                             identity=self.ident[:])
            if kc == 7:
                kb.op("pe", f, reads=[xb, self.cb], writes=[pb])
            else:
                kb_op_noinc(kb, "pe", f, reads=[xb, self.cb], writes=[pb])
        kb.op("dve", lambda g: g.tensor_tensor(out=uT_view, in0=ptb.rearrange("p (k t) -> p k t", k=8),
                                               in1=gcol.to_broadcast([128, 8, 128]), op=ALU.mult),
              reads=[pb, self.cb], writes=[ubuf])

    def postnorm_res(self, halves, h_in, hinb, h_out, houtb, gbc):
        kb = self.kb
        junk, jb = self.junk.get()
        st, sb_ = self.st.get()
        for i, (pa, pb) in enumerate(halves):
            kb.op("act", lambda g, i=i, pa=pa: g.activation(out=junk[:, i * 512:(i + 1) * 512], in_=pa, func=AF.Square,
                                                           accum_out=st[:, i:i + 1]), reads=[pb], writes=[jb, sb_])
        kb.op("dve", lambda g: g.tensor_tensor(out=st[:, 0:1], in0=st[:, 0:1], in1=st[:, 1:2], op=ALU.add),
              reads=[sb_], writes=[sb_])
        self.rstd_from_ss(st, sb_, 1, float(D))
        tmp, tb = self.tmp.get()
        for i, (pa, pb) in enumerate(halves):
            kb.op("dve", lambda g, i=i, pa=pa: g.scalar_tensor_tensor(out=tmp[:, i * 512:(i + 1) * 512], in0=pa,
                                                                     scalar=st[:, 4:5], in1=gbc[:, i * 512:(i + 1) * 512],
                                                                     op0=ALU.mult, op1=ALU.mult),
                  reads=[pb, sb_, self.cb], writes=[tb])
        kb.op("pool", lambda g: g.tensor_tensor(out=h_out, in0=tmp[:], in1=h_in, op=ALU.add),
              reads=[tb, hinb], writes=[houtb])


def wview(w):
    return w.rearrange("(kc p) n -> p kc n", p=128)


class FFN:
    def __init__(self, kb, dn, NT=4):
        self.kb, self.dn, self.NT = kb, dn, NT
        self.wg = Rot(kb, "wg", [128, 8, 256], BF16, 2)
        self.wu = Rot(kb, "wu", [128, 8, 256], BF16, 2)
        self.wd = kb.sb("wd", [128, 22, 1024], BF16)
        self.wdb = [Buf("wd%d" % i) for i in range(11)]
        self.actT = kb.sb("actT", [128, 22, NT * 128], BF16)
        self.actb = [Buf("act%d" % i) for i in range(22)]
        self.sg = Rot(kb, "sg", [128, 512], F32, 2)

    def emit(self, uT, ubufs, w_gate, w_up, w_down, gbc_out, h_dram, hrot, after, side=None):
        kb, NT = self.kb, self.NT
        ntok = NT * 128
        wdv = wview(w_down)
        for j in range(11):
            kb.dma("sp", self.wd[:, 2 * j:2 * j + 2, :], wdv[:, 2 * j:2 * j + 2, :], writes=[self.wdb[j]])
        blocks = [(c, 256) for c in range(0, DFF, 256)]
        for (c0, wd_) in blocks:
            wg, wgb = self.wg.get()
            wu, wub = self.wu.get()
            if FAKE_CONTIG:
                bi = c0 // 256
                kb.dma("sp", wg[:, :, 0:wd_], w_gate.rearrange("(p a) n -> p (a n)", p=128)[:, bi * 2048:(bi + 1) * 2048].rearrange("p (k n) -> p k n", k=8), writes=[wgb])
                kb.dma("sp", wu[:, :, 0:wd_], w_up.rearrange("(p a) n -> p (a n)", p=128)[:, bi * 2048:(bi + 1) * 2048].rearrange("p (k n) -> p k n", k=8), writes=[wub])
            else:
                kb.dma("sp", wg[:, :, 0:wd_], wview(w_gate)[:, :, c0:c0 + wd_], writes=[wgb])
                kb.dma("sp", wu[:, :, 0:wd_], wview(w_up)[:, :, c0:c0 + wd_], writes=[wub])
            for oc in range(wd_ // 128):
                occ = c0 // 128 + oc
                pg, pgb = kb.ps()
                mm_group(kb, pg, pgb, pg[:, 0:ntok],
                         [(wg[:, kc, oc * 128:(oc + 1) * 128], uT[:, kc, :]) for kc in range(8)], reads=[wgb] + ubufs)
                pu, pub = kb.ps()
                mm_group(kb, pu, pub, pu[:, 0:ntok],
                         [(wu[:, kc, oc * 128:(oc + 1) * 128], uT[:, kc, :]) for kc in range(8)], reads=[wub] + ubufs)
                sg, sgb = self.sg.get()
                kb.op("act", lambda g: g.activation(out=sg[:, 0:ntok], in_=pg[:, 0:ntok], func=AF.Silu), reads=[pgb], writes=[sgb])
                kb.op("dve", lambda g: g.tensor_tensor(out=self.actT[:, occ, :], in0=sg[:, 0:ntok], in1=pu[:, 0:ntok], op=ALU.mult),
                      reads=[sgb, pub], writes=[self.actb[occ]])
        def down_stage():
            for t in range(NT):
                halves = []
                for oh in range(2):
                    py, pyb = kb.ps()
                    mm_group(kb, py, pyb, py[:, :],
                             [(self.actT[:, kc, t * 128:(t + 1) * 128], self.wd[:, kc, oh * 512:(oh + 1) * 512]) for kc in range(22)],
                             reads=self.actb + self.wdb)
                    halves.append((py[:, :], pyb))
                hi, hib = hrot.get()
                kb.dma("sp", hi[:], h_dram[t * 128:(t + 1) * 128, :], writes=[hib])
                ho, hob = hrot.get()
                self.dn.postnorm_res(halves, hi[:], hib, ho[:], hob, gbc_out)
                after(t, ho, hob)

        interleave(kb, [down_stage, side])


TPC = 2048
NPASS = TPC // 512
C_Q, C_K, C_V, C_Z, C_X, C_DT, C_G = 0, 1536, 3072, 4608, 6656, 10752, 10784


def build_A(with_win=True):
    kb = KB()
    kb.psum_banks()
    T = {"h": kb.din("h", [TPC, D], F32), "ident": kb.din("ident", [128, 128], F32), "g0col": kb.din("g0col", [128, 8], F32),
         "g2col": kb.din("g2col", [128, 8], F32), "g1bc": kb.din("g1bc", [128, D], F32), "bgcol": kb.din("bgcol", [128, 16], F32),
         "w_gate": kb.din("w_gate", [D, DFF], BF16), "w_up": kb.din("w_up", [D, DFF], BF16), "w_down": kb.din("w_down", [DFF, D], BF16),
         "w_in": kb.din("w_in", [D, INW], BF16), "h1": kb.dout("h1", [TPC, D], F32)}
    if with_win:
        T["u2T"] = kb.dint("s_u2T", [D, TPC], BF16)
        T.update({"qkT": kb.dout("qkT", [3072, TPC], BF16), "v": kb.dout("v", [TPC, 1536], BF16), "z": kb.dout("z", [TPC, 2048], F32),
                  "xbcT": kb.dout("xbcT", [4096, TPC], F32), "dt": kb.dout("dt", [TPC, 32], F32), "gT": kb.dout("gT", [2048, TPC], F32)})
    emit_A(kb, T, TPC, with_win)
    return kb.finish()


def emit_A(kb, T, ntok, with_win=True):
    h, ident_d, g0col_d, g2col_d, g1bc_d, bgcol_d = T["h"], T["ident"], T["g0col"], T["g2col"], T["g1bc"], T["bgcol"]
    w_gate, w_up, w_down, w_in, h1 = T["w_gate"], T["w_up"], T["w_down"], T["w_in"], T["h1"]
    if with_win:
        qkT, v_o, z_o, xbcT, dt_o, gT = T["qkT"], T["v"], T["z"], T["xbcT"], T["dt"], T["gT"]
    toks = []
    dn = Dense(kb, ident_d)
    g0col = dn.load_const("g0col", [128, 8], g0col_d)
    g2col = dn.load_const("g2col", [128, 8], g2col_d)
    g1bc = dn.load_const("g1bc", [128, D], g1bc_d)
    bgcol = dn.load_const("bgcol", [128, 16], bgcol_d)
    kb.op("dve", lambda g: g.tensor_scalar(out=g1bc[:], in0=g1bc[:], scalar1=0.5, scalar2=None, op0=ALU.mult),
          reads=[dn.cb], writes=[dn.cb])
    ffn = FFN(kb, dn)
    hrot = Rot(kb, "hrot", [128, D], F32, 4)
    uTs = [kb.sb("uT_%d" % i, [128, 8, 512], BF16) for i in range(2)]
    ubs = [[Buf("uT%d_%d" % (i, j)) for j in range(4)] for i in range(2)]
    kb.default_banks = list(range(7))
    SIDE = ([7], "side")
    uT2 = kb.sb("uT2", [128, 8, 512], BF16)
    ub2 = [Buf("uT2%d" % i) for i in range(4)]
    win = Rot(kb, "win", [128, 8, 512], BF16, 2)
    stf = Rot(kb, "stf", [128, 512], F32, 4)
    stb = Rot(kb, "stb", [128, 512], BF16, 4)
    winv = wview(w_in)

    dn_s = Dense(kb, ident_d, tag="s")
    g0col_s = kb.sb("g0col_s", [128, 8], F32)
    kb.dma("sp", g0col_s[:], g0col_d, writes=[dn_s.cb], key="d_consts")

    def pre_norm(p):
        for t in range(4):
            hi, hib = hrot2.get()
            kb.dma(HQ, hi[:], h[p * 512 + t * 128:p * 512 + (t + 1) * 128, :], writes=[hib])
            dn_s.norm_T(hi[:], hib, g0col_s[:, 0:8], uTs[p % 2][:, :, t * 128:(t + 1) * 128], ubs[p % 2][t])

    def pre_norm_side(p):
        kb.tls.pool = SIDE
        pre_norm(p)

    hrot2 = Rot(kb, "hrot2", [128, D], F32, 2)
    pre_norm(0)
    npass = ntok // 512
    for p in range(npass):
        t0 = p * 512
        uT, ub = uTs[p % 2], ubs[p % 2]

        def after(t, ho, hob, t0=t0):
            toks.append(kb.dma("pool", h1[t0 + t * 128:t0 + (t + 1) * 128, :], ho[:], reads=[hob]))
            if with_win:
                dn.norm_T(ho[:], hob, g2col[:, 0:8], uT2[:, :, t * 128:(t + 1) * 128], ub2[t])

        nxt_norm = (lambda p=p: pre_norm_side(p + 1)) if p + 1 < npass else None
        if not with_win:
            ffn.emit(uT, ub, w_gate, w_up, w_down, g1bc, h[t0:t0 + 512, :], hrot, after, side=nxt_norm)
            continue
        ffn.emit(uT, ub, w_gate, w_up, w_down, g1bc, h[t0:t0 + 512, :], hrot, after, side=nxt_norm)
        kb.dma("pool", T["u2T"].rearrange("(kc p) t -> p kc t", p=128)[:, :, t0:t0 + 512], uT2[:], reads=ub2, key="d_u2T")
        continue

        def load_w(c0, wd_):
            w, wb = win.get()
            kb.dma("sp", w[:, :, 0:wd_], winv[:, :, c0:c0 + wd_], writes=[wb])
            return w, wb

        def fm_block(c0, wd_, dst, r0, dt_out, bias=None):
            w, wb = load_w(c0, wd_)
            for oc in range(wd_ // 128):
                pp, ppb = kb.ps()
                mm_group(kb, pp, ppb, pp[:, :], [(w[:, kc, oc * 128:(oc + 1) * 128], uT2[:, kc, :]) for kc in range(8)],
                         reads=[wb] + ub2)
                st, sb_ = (stb if dt_out == BF16 else stf).get()
                if bias is not None:
                    j = (r0 + oc * 128) // 128
                    kb.op("act", lambda g: g.activation(out=st[:], in_=pp[:, :], func=AF.Sigmoid, bias=bgcol[:, j:j + 1], scale=1.0),
                          reads=[ppb, dn.cb], writes=[sb_])
                elif oc % 2 == 0:
                    kb.op("act", lambda g: g.copy(out=st[:], in_=pp[:, :]), reads=[ppb], writes=[sb_])
                else:
                    kb.op("dve", lambda g: g.tensor_copy(out=st[:], in_=pp[:, :]), reads=[ppb], writes=[sb_])
                rr = r0 + oc * 128
                toks.append(kb.dma("pool", dst[rr:rr + 128, t0:t0 + 512], st[:], reads=[sb_]))

        def tm_block(c0, wd_, dst, cdst, dt_out):
            w, wb = load_w(c0, wd_)
            for t in range(4):
                pp, ppb = kb.ps()
                mm_group(kb, pp, ppb, pp[:, 0:wd_], [(uT2[:, kc, t * 128:(t + 1) * 128], w[:, kc, 0:wd_]) for kc in range(8)],
                         reads=[wb, ub2[t]])
                st, sb_ = (stb if dt_out == BF16 else stf).get()
                if t % 2 == 0:
                    kb.op("act", lambda g: g.copy(out=st[:, 0:wd_], in_=pp[:, 0:wd_]), reads=[ppb], writes=[sb_])
                else:
                    kb.op("dve", lambda g: g.tensor_copy(out=st[:, 0:wd_], in_=pp[:, 0:wd_]), reads=[ppb], writes=[sb_])
                toks.append(kb.dma("pool", dst[t0 + t * 128:t0 + (t + 1) * 128, cdst:cdst + wd_], st[:, 0:wd_], reads=[sb_]))

        def win_stage():
            for c in range(0, 1536, 512):
                tm_block(C_V + c, 512, v_o, c, BF16)
            for c in range(0, 2048, 512):
                tm_block(C_Z + c, 512, z_o, c, F32)
            tm_block(C_DT, 32, dt_o, 0, F32)
            for c in range(0, 3072, 512):
                fm_block(c, 512, qkT, c, BF16)
            for c in range(0, 4096, 512):
                fm_block(C_X + c, 512, xbcT, c, F32)
            for c in range(0, 2048, 512):
                fm_block(C_G + c, 512, gT, c, F32, bias=True)

        interleave(kb, [win_stage, nxt_norm])
    kb.default_banks = list(range(8))
    if not with_win:
        return toks
    kb.end_phase()
    kb.begin_phase()
    u2T = T["u2T"]
    uall = kb.sb("uall", [128, 8, ntok], BF16)
    uallb = Buf("uall")
    kb.dma_multi("sp", [(uall[:, kc, :], u2T[kc * 128:(kc + 1) * 128, :]) for kc in range(8)], writes=[uallb])
    bgc = kb.sb("bgc2", [128, 16], F32)
    bgb = Buf("bgc2")
    kb.dma("sp", bgc[:], bgcol_d, writes=[bgb])
    win2 = Rot(kb, "win2", [128, 8, 512], BF16, 3)
    stf2 = Rot(kb, "stf2", [128, 512], F32, 4)
    stb2 = Rot(kb, "stb2", [128, 512], BF16, 4)
    ntb = ntok // 512

    def load_w2(c0, wd_):
        w, wb_ = win2.get()
        kb.dma("sp", w[:, :, 0:wd_], winv[:, :, c0:c0 + wd_], writes=[wb_])
        return w, wb_

    cnt = [0]

    def evac(st, sb_, src, pb_, bias_j=None):
        if bias_j is not None:
            kb.op("act", lambda g: g.activation(out=st, in_=src, func=AF.Sigmoid, bias=bgc[:, bias_j:bias_j + 1], scale=1.0),
                  reads=[pb_, bgb], writes=[sb_])
        elif cnt[0] % 2 == 0:
            kb.op("act", lambda g: g.copy(out=st, in_=src), reads=[pb_], writes=[sb_])
        else:
            kb.op("dve", lambda g: g.tensor_copy(out=st, in_=src), reads=[pb_], writes=[sb_])
        cnt[0] += 1

    def fm2(c0, wd_, dst, r0, dt_out, bias=False):
        w, wb_ = load_w2(c0, wd_)
        for oc in range(wd_ // 128):
            rr = r0 + oc * 128
            for tb in range(ntb):
                ts_ = slice(tb * 512, (tb + 1) * 512)
                pp, ppb = kb.ps()
                mm_group(kb, pp, ppb, pp[:, :], [(w[:, kc, oc * 128:(oc + 1) * 128], uall[:, kc, ts_]) for kc in range(8)], reads=[wb_, uallb])
                st, sb_ = (stb2 if dt_out == BF16 else stf2).get()
                evac(st[:], sb_, pp[:, :], ppb, bias_j=(rr // 128 if bias else None))
                toks.append(kb.dma("pool", dst[rr:rr + 128, ts_], st[:], reads=[sb_]))

    def tm2(c0, wd_, dst, cdst, dt_out):
        w, wb_ = load_w2(c0, wd_)
        for t in range(ntok // 128):
            pp, ppb = kb.ps()
            mm_group(kb, pp, ppb, pp[:, 0:wd_], [(uall[:, kc, t * 128:(t + 1) * 128], w[:, kc, 0:wd_]) for kc in range(8)], reads=[wb_, uallb])
            st, sb_ = (stb2 if dt_out == BF16 else stf2).get()
            evac(st[:, 0:wd_], sb_, pp[:, 0:wd_], ppb)
            toks.append(kb.dma("pool", dst[t * 128:(t + 1) * 128, cdst:cdst + wd_], st[:, 0:wd_], reads=[sb_]))

    for c in range(0, 3072, 512):
        fm2(c, 512, qkT, c, BF16)
    for c in range(0, 1536, 512):
        tm2(C_V + c, 512, v_o, c, BF16)
    for c in range(0, 2048, 512):
        tm2(C_Z + c, 512, z_o, c, F32)
    for c in range(0, 4096, 512):
        fm2(C_X + c, 512, xbcT, c, F32)
    tm2(C_DT, 32, dt_o, 0, F32)
    for c in range(0, 2048, 512):
        fm2(C_G + c, 512, gT, c, F32, bias=True)
    return toks
DILS = (1, 4, 16)


def build_M1(dbg=0):
    kb = KB()
    kb.psum_banks()
    S = SEQ
    qT = kb.din("qT", [768, S], BF16)
    kT = kb.din("kT", [768, S], BF16)
    v = kb.din("v", [S, 768], BF16)
    mask_d = kb.din("mask", [128, 512], F32)
    yT = kb.dout("yattT", [256, S], BF16)
    emit_M1(kb, lambda g: qT[g * 256:(g + 1) * 256, :], lambda g: kT[g * 256:(g + 1) * 256, :],
            lambda g: v[:, g * 256:(g + 1) * 256], mask_d, yT)
    return kb.finish()


def emit_M1(kb, qrows, krows, vcols, mask_d, yT, dbg=0, side=None):
    S = SEQ
    toks = []
    cb = Buf("consts")
    mask_f = kb.sb("mask_f", [128, 512], F32)
    mask = kb.sb("mask_b", [128, 512], BF16)
    ones_f = kb.sb("ones_f", [128, 64], F32)
    kb.dma("sp", mask_f[:], mask_d, writes=[cb], key="d_const")
    kb.op("dve", lambda g: g.tensor_copy(out=mask[:], in_=mask_f[:]), reads=[cb], writes=[cb])
    kb.op("dve", lambda g: g.memset(ones_f[:], 0.0), writes=[cb])
    kb.op("dve", lambda g: g.memset(ones_f[64:65, :], 1.0), writes=[cb])
    qs = kb.sb("qs", [128, 2, S], BF16)
    ks = kb.sb("ks", [128, 2, S], BF16)
    qb, kbuf = Buf("qs"), Buf("ks")
    vst = kb.sb("vst", [128, 32, 256], BF16)
    vstb = Buf("vst")
    vaug = kb.sb("vaug", [128, 32, 4, 65], BF16)
    vab = Buf("vaug")
    tot = kb.sb("tot", [128, 4, S], F32)
    totb = Buf("tot")
    prot = Rot(kb, "P", [128, 512], BF16, 6)
    yst = Rot(kb, "yst", [64, 512], BF16, 3)
    PS_S = ([0, 1, 2, 3], "s")
    PS_O = ([4, 5, 6, 7], "o")
    side = list(side) if side else []
    nseg = 96
    per_seg = (len(side) + nseg - 1) // nseg if side else 0
    if side:
        CH = max(w_ for (_, _, w_) in side)
        cin = Rot(kb, "cin", [128, CH], F32, 3)
        cout = Rot(kb, "cout", [128, CH], BF16, 3)
    seg_ctr = [0]

    def side_fn():
        k0 = seg_ctr[0] * per_seg
        seg_ctr[0] += 1
        items = side[k0:k0 + per_seg]
        if not items:
            return None

        def run_():
            for (src, dst, w_) in items:
                ti, bi = cin.get()
                to, bo = cout.get()
                kb.dma("sp", ti[:, 0:w_], src, writes=[bi])
                kb.op("pool", lambda g_: g_.tensor_copy(out=to[:, 0:w_], in_=ti[:, 0:w_]), reads=[bi], writes=[bo])
                kb.dma("pool", dst, to[:, 0:w_], reads=[bo])
        return run_
    kb.op("pool", lambda g: g.memset(vaug[:], 1.0), writes=[vab])

    for g in range(3):
        d = DILS[g]
        nb = S // (128 * d)
        kb.dma("sp", qs[:], qrows(g).rearrange("(c p) t -> p c t", p=128), writes=[qb])
        kb.dma("sp", ks[:], krows(g).rearrange("(c p) t -> p c t", p=128), writes=[kbuf])
        vv = vcols(g).rearrange("(n i r) f -> r i n f", i=128, r=d)
        pairs = []
        for r in range(d):
            for n0 in range(0, nb, 8):
                n1 = min(nb, n0 + 8)
                pairs.append((vst[:, r * nb + n0:r * nb + n1, :], vv[r][:, n0:n1, :]))
        kb.dma_multi("sp", pairs, writes=[vstb])
        kb.op("pool", lambda g_: g_.tensor_copy(out=vaug[:, :, :, 0:64], in_=vst[:].rearrange("p b (h e) -> p b h e", h=4)),
              reads=[vstb], writes=[vab])

        def tk(r, n):
            s0 = r + d * 128 * n
            return slice(s0, s0 + d * 127 + 1, d)

        if dbg == 1 and g == 0:
            kb.op("dve", lambda g_: g_.memset(tot[:], 1.0), writes=[totb])
        def make_block(r, n):
            blk = r * nb + n
            pss = [kb.ps(PS_S), kb.ps(PS_S)]
            Ps = [prot.get(), prot.get()]
            po, pob = kb.ps(PS_O)

            def E():
                for ch in range(2):
                    for hp in range(2):
                        pst, psb = pss[hp]
                        pl = slice(hp * 64, (hp + 1) * 64)
                        if n > 0:
                            kb_op_noinc(kb, "pe", lambda g_: g_.matmul(pst[:, ch * 256:ch * 256 + 128], lhsT=ks[pl, ch, tk(r, n - 1)],
                                                                      rhs=qs[pl, ch, tk(r, n)], start=True, stop=True),
                                        reads=[qb, kbuf], writes=[psb])
                        f = lambda g_: g_.matmul(pst[:, ch * 256 + 128:ch * 256 + 256], lhsT=ks[pl, ch, tk(r, n)],
                                                 rhs=qs[pl, ch, tk(r, n)], start=True, stop=True)
                        if ch == 1:
                            kb.op("pe", f, reads=[qb, kbuf], writes=[psb])
                        else:
                            kb_op_noinc(kb, "pe", f, reads=[qb, kbuf], writes=[psb])
                for hp in range(2):
                    pst, psb = pss[hp]
                    P, Pb = Ps[hp]
                    if n > 0:
                        sv, pv, mv = pst[:, :], P[:], mask[:]
                    else:
                        sel = lambda a: a.rearrange("p (h c) -> p h c", h=2)[:, :, 128:256]
                        sv, pv, mv = sel(pst[:, :]), sel(P[:]), sel(mask[:])
                    kb.op("act", lambda g_: g_.activation(out=pv, in_=sv, func=AF.Exp, scale=0.125), reads=[psb], writes=[Pb])
                    kb.op("dve", lambda g_: g_.tensor_tensor(out=pv, in0=pv, in1=mv, op=ALU.mult), reads=[Pb, cb], writes=[Pb])

            def Lt():
                for hp in range(2):
                    P, Pb = Ps[hp]
                    for ch in range(2):
                        hd = 2 * ch + hp
                        oap = po[0:65, hd * 128:(hd + 1) * 128]
                        if n > 0:
                            kb_op_noinc(kb, "pe", lambda g_: g_.matmul(oap, lhsT=vaug[:, blk - 1, hd, :], rhs=P[:, ch * 256:ch * 256 + 128],
                                                                      start=True, stop=False), reads=[Pb, vab], writes=[pob])
                        f = lambda g_: g_.matmul(oap, lhsT=vaug[:, blk, hd, :], rhs=P[:, ch * 256 + 128:ch * 256 + 256],
                                                 start=(n == 0), stop=True)
                        if hp == 1 and ch == 1:
                            kb.op("pe", f, reads=[Pb, vab], writes=[pob])
                        else:
                            kb_op_noinc(kb, "pe", f, reads=[Pb, vab], writes=[pob])
                tv = tot[0:65, :, tk(r, n)]
                pov = po[0:65, :].rearrange("p (h q) -> p h q", h=4)
                if g == 0:
                    kb.op("dve", lambda g_: g_.tensor_copy(out=tv, in_=pov), reads=[pob], writes=[totb])
                else:
                    kb.op("dve", lambda g_: g_.tensor_tensor(out=tv, in0=tv, in1=pov, op=ALU.add), reads=[pob, totb], writes=[totb])

            return E, Lt

        prevLt = None
        for r in range(d):
            for n in range(nb):
                E, Lt = make_block(r, n)
                interleave(kb, [E, prevLt, side_fn()])
                prevLt = Lt
        interleave(kb, [prevLt])
    den = tot[64:65, :, :]
    for hd in range(4):
        for c0 in range(0, S, 2048):
            dv = tot[64:65, hd, c0:c0 + 2048]
            kb.op("act", lambda g_: g_.activation(out=dv, in_=dv, func=AF.Ln), reads=[totb], writes=[totb])
            kb.op("act", lambda g_: g_.activation(out=dv, in_=dv, func=AF.Exp, scale=-1.0), reads=[totb], writes=[totb])
    for hd in range(4):
        for tb in range(S // 512):
            pb_, pbb = kb.ps(PS_S)
            ts_ = slice(tb * 512, (tb + 1) * 512)
            kb.op("pe", lambda g_: g_.matmul(pb_[0:64, :], lhsT=ones_f[0:65, 0:64], rhs=tot[0:65, hd, ts_], start=True, stop=True),
                  reads=[totb, cb], writes=[pbb])
            ys, ysb = yst.get()
            kb.op("dve", lambda g_: g_.tensor_tensor(out=ys[:], in0=tot[0:64, hd, ts_], in1=pb_[0:64, :], op=ALU.mult),
                  reads=[totb, pbb], writes=[ysb])
            toks.append(kb.dma("pool", yT[hd * 64:(hd + 1) * 64, ts_], ys[:], reads=[ysb]))
    return toks


def attn_mask():
    j = np.arange(128)[:, None]
    i = np.arange(128)[None, :]
    prev = (j >= i).astype(np.float32)
    cur = (j <= i).astype(np.float32)
    return np.ascontiguousarray(np.concatenate([prev, cur, prev, cur], axis=1))


def build_M2():
    kb = KB()
    kb.psum_banks()
    S = SEQ
    xbcT = kb.din("xbcT", [2048, S], F32)
    T = {"xrow": lambda c: xbcT[c * 128:(c + 1) * 128, :], "z": kb.din("z", [S, 1024], F32), "dt": kb.din("dt", [S, 16], F32),
         "cw": kb.din("cw", [128, 64], F32), "cbias": kb.din("cbias", [128, 16], F32), "dtb": kb.din("dtb", [128, 16], F32),
         "alog": kb.din("alog", [128, 16], F32), "dsk": kb.din("dsk", [128, 16], F32), "ng": kb.din("ng", [128, 1024], F32),
         "U": kb.din("U", [128, 128], F32), "ident": kb.din("ident", [128, 128], F32), "yT": kb.dout("yssdT", [1024, S], BF16)}
    emit_M2(kb, T)
    return kb.finish()


def emit_M2(kb, T, side=None):
    S = SEQ
    xrow, z_d, dt_d, cw_d, cbias_d, dtb_d, alog_d, dsk_d, ng_d, U_d, ident_d, yT_o = (
        T["xrow"], T["z"], T["dt"], T["cw"], T["cbias"], T["dtb"], T["alog"], T["dsk"], T["ng"], T["U"], T["ident"], T["yT"])
    toks = []
    cb = Buf("consts")

    def cload(name, shape, ap):
        t = kb.sb("c_" + name, shape, F32)
        kb.dma("sp", t[:], ap, writes=[cb], key="d_const")
        return t

    cw = cload("cw", [128, 64], cw_d)
    cbias = cload("cbias", [128, 16], cbias_d)
    dtb = cload("dtb", [128, 16], dtb_d)
    Aneg = cload("alog", [128, 16], alog_d)
    dsk = cload("dsk", [128, 16], dsk_d)
    ng = cload("ng", [128, 1024], ng_d)
    U = cload("U", [128, 128], U_d)
    ident_f = cload("ident", [128, 128], ident_d)
    ident = kb.sb("ident_b", [128, 128], BF16)
    ones = kb.sb("ones_f", [128, 128], F32)
    kb.op("dve", lambda g: g.tensor_copy(out=ident[:], in_=ident_f[:]), reads=[cb], writes=[cb])
    kb.op("dve", lambda g: g.memset(ones[:], 1.0), writes=[cb])
    negf = kb.sb("negf", [128, 128], F32)
    neg = kb.sb("neg_b", [128, 512], BF16)
    kb.op("dve", lambda g: g.tensor_scalar(out=negf[:], in0=U[:], scalar1=30000.0, scalar2=-30000.0, op0=ALU.mult, op1=ALU.add),
          reads=[cb], writes=[cb])
    kb.op("dve", lambda g: g.tensor_copy(out=neg[:].rearrange("p (q l) -> p q l", q=4), in_=negf[:].unsqueeze(1).to_broadcast([128, 4, 128])),
          reads=[cb], writes=[cb])
    kb.op("act", lambda g: g.activation(out=Aneg[:], in_=Aneg[:], func=AF.Exp), reads=[cb], writes=[cb])
    kb.op("dve", lambda g: g.tensor_scalar(out=Aneg[:], in0=Aneg[:], scalar1=-1.0, scalar2=None, op0=ALU.mult), reads=[cb], writes=[cb])

    xin = Rot(kb, "xin", [128, 515], F32, 4)
    acc = Rot(kb, "acc", [128, 512], F32, 2)
    xc_r = Rot(kb, "xc", [128, 16, 512], BF16, 2)
    xtm_r = Rot(kb, "xtm", [128, 1024], BF16, 2)
    btm_r = Rot(kb, "btm", [128, 512], BF16, 2)
    xdt_r = Rot(kb, "xdt", [128, 1024], BF16, 2)
    xdd_r = Rot(kb, "xdd", [128, 1024], BF16, 2)
    z_r = Rot(kb, "zc", [128, 4, 1024], F32, 2)
    cbm_r = Rot(kb, "cbm", [128, 512], F32, 2)
    L_r = Rot(kb, "Lm", [128, 512], F32, 2)
    yb_r = Rot(kb, "yb", [128, 1024], F32, 2)
    t4_r = Rot(kb, "t4", [128, 1024], F32, 2)
    yn_r = Rot(kb, "yn", [128, 1024], BF16, 2)
    junk_r = Rot(kb, "junk", [128, 256], BF16, 2)
    yT_r = Rot(kb, "yTs", [128, 8, 512], BF16, 2)
    dtr_r = Rot(kb, "dtr", [128, 4, 16], F32, 2)
    dtw_r = Rot(kb, "dtw", [128, 6, 64], F32, 2)
    sm_r = Rot(kb, "sm", [128, 8, 16], F32, 3)
    prev = kb.sb("prev", [128, 1024], F32)
    prev_bf = kb.sb("prev_bf", [128, 1024], BF16)
    ptmp = kb.sb("ptmp", [128, 1024], F32)
    pvb, pbb, ptb_ = Buf("prev"), Buf("prev_bf"), Buf("ptmp")
    kb.op("dve", lambda g: g.memset(prev[:], 0.0), writes=[pvb])
    kb.op("dve", lambda g: g.memset(prev_bf[:], 0.0), writes=[pbb])

    B_XT, B_BT, B_CB, B_R, B_PY, B_PO, B_PS, B_CV = range(8)
    B_YT = B_XT
    xcbuf_sets = [[Buf("xc%d_%d" % (i, c)) for c in range(16)] for i in range(2)]
    Wall_r = Rot(kb, "Wall", [128, 16 * 128], BF16, 2)
    Wbuf_sets = [[Buf("W%d_%d" % (i, g)) for g in range(4)] for i in range(2)]

    def bank(i):
        return kb.pbank[i]

    h3 = lambda ap: ap.rearrange("p (h e) -> p h e", e=64)
    bch = lambda ap, lo, n: ap[:, lo:lo + n].to_broadcast([128, n, 64])
    q3 = lambda ap: ap.rearrange("p (q l) -> p q l", q=4)
    noinc = lambda *a_, **k_: kb_op_noinc(kb, *a_, **k_)

    def prologue(tb):
        t0 = tb * 512
        xc, xcb = xc_r.get()
        xcbufs = xcbuf_sets[tb % 2]
        for c in range(16):
            xi, xib = xin.get()
            if tb == 0:
                kb.op("pool", lambda g: g.memset(xi[:, 0:3], 0.0), writes=[xib])
                kb.dma("sp", xi[:, 3:515], xrow(c)[:, 0:512], writes=[xib])
            else:
                kb.dma("sp", xi[:], xrow(c)[:, t0 - 3:t0 + 512], writes=[xib])
            ac, acb = acc.get()
            kb.op("dve", lambda g: g.tensor_scalar(out=ac[:], in0=xi[:, 3:515], scalar1=cw[:, c * 4 + 3:c * 4 + 4],
                                                   scalar2=cbias[:, c:c + 1], op0=ALU.mult, op1=ALU.add),
                  reads=[xib, cb], writes=[acb])
            for k in (2, 1, 0):
                kb.op("dve", lambda g, k=k: g.scalar_tensor_tensor(out=ac[:], in0=xi[:, k:k + 512], scalar=cw[:, c * 4 + k:c * 4 + k + 1],
                                                                   in1=ac[:], op0=ALU.mult, op1=ALU.add),
                      reads=[xib, cb, acb], writes=[acb])
            kb.op("act", lambda g: g.activation(out=xc[:, c, :], in_=ac[:], func=AF.Silu), reads=[acb], writes=[xcbufs[c]])
        zc, zcb = z_r.get()
        kb.dma_multi("sp", [(zc[:, j, :], z_d[t0 + j * 128:t0 + (j + 1) * 128, :]) for j in range(4)], writes=[zcb])
        for j in range(4):
            kb.op("act", lambda g, j=j: g.activation(out=zc[:, j, :], in_=zc[:, j, :], func=AF.Silu), reads=[zcb], writes=[zcb])
        dtr, dtrb = dtr_r.get()
        kb.dma("sp", dtr[:], dt_d[t0:t0 + 512, :].rearrange("(j p) h -> p j h", p=128), writes=[dtrb])
        dw, dwb = dtw_r.get()
        T_, AX_, E_, L_, DT_, A_ = [dw[:, i, :] for i in range(6)]
        v3 = lambda ap: ap.rearrange("p (j h) -> p j h", j=4)
        kb.op("dve", lambda g: g.tensor_tensor(out=v3(T_), in0=dtr[:], in1=dtb[:].unsqueeze(1).to_broadcast([128, 4, 16]), op=ALU.add),
              reads=[dtrb, cb], writes=[dwb])
        kb.op("act", lambda g: g.activation(out=AX_, in_=T_, func=AF.Abs), reads=[dwb], writes=[dwb])
        kb.op("act", lambda g: g.activation(out=E_, in_=AX_, func=AF.Exp, scale=-1.0), reads=[dwb], writes=[dwb])
        kb.op("act", lambda g: g.activation(out=L_, in_=E_, func=AF.Ln, bias=1.0, scale=1.0), reads=[dwb], writes=[dwb])
        kb.op("dve", lambda g: g.scalar_tensor_tensor(out=DT_, in0=T_, scalar=0.0, in1=L_, op0=ALU.max, op1=ALU.add),
              reads=[dwb], writes=[dwb])
        kb.op("dve", lambda g: g.tensor_tensor(out=v3(A_), in0=v3(DT_), in1=Aneg[:].unsqueeze(1).to_broadcast([128, 4, 16]), op=ALU.mult),
              reads=[dwb, cb], writes=[dwb])
        yTs, yTb = yT_r.get()
        return dict(xc=xc, xall=xcbufs, zc=zc, zcb=zcb, DT_=DT_, A_=A_, dwb=dwb, yTs=yTs, yTb=yTb, t0=t0)

    def make_chunk(B, j, J):
        xc, xall, zc, zcb, dwb, yTs, yTb = B["xc"], B["xall"], B["zc"], B["zcb"], B["dwb"], B["yTs"], B["yTb"]
        js = slice(j * 128, (j + 1) * 128)
        hs = slice(j * 16, (j + 1) * 16)
        dt_j = B["DT_"][:, hs]
        a_j = B["A_"][:, hs]
        xtm, xtmb = xtm_r.get()
        btm, btmb = btm_r.get()
        sm, smb = sm_r.get()
        xdt, xdtb = xdt_r.get()
        xdd, xddb = xdd_r.get()
        Wall, _ = Wall_r.get()
        Wbs = Wbuf_sets[J % 2]
        NACS, DOUT, DSTS, WST, CD, TMP, SS, RS = [sm[:, i, :] for i in range(8)]

        def E():
            pxt, pxtb = bank(B_XT)
            pxv = pxt[:].bitcast(BF16)
            with kb.atomic():
                for c in range(8):
                    f = lambda g, c=c: g.transpose(out=pxv[:, c * 128:(c + 1) * 128], in_=xc[:, c, js], identity=ident[:])
                    (kb.op if c == 7 else noinc)("pe", f, reads=[xall[c], cb], writes=[pxtb])
                kb.op("act", lambda g: g.copy(out=xtm[:], in_=pxv), reads=[pxtb], writes=[xtmb])
            if kb.hook is not None:
                kb.hook()
            pbt, pbtb = bank(B_BT)
            pbv = pbt[:].bitcast(BF16)
            for c in range(4):
                f = lambda g, c=c: g.transpose(out=pbv[:, c * 128:(c + 1) * 128], in_=xc[:, 8 + c, js], identity=ident[:])
                noinc("pe", f, reads=[xall[8 + c], cb], writes=[pbtb])
            noinc("pe", lambda g: g.matmul(pbt[:, 256:272], lhsT=U[:], rhs=a_j, start=True, stop=True), reads=[dwb, cb], writes=[pbtb])
            kb.op("pe", lambda g: g.matmul(pbt[:, 272:288], lhsT=ones[:], rhs=a_j, start=True, stop=True), reads=[dwb, cb], writes=[pbtb])
            kb.op("dve", lambda g: g.tensor_copy(out=btm[:], in_=pbv[:, 0:512]), reads=[pbtb], writes=[btmb])
            acs_p, tot_p = pbt[:, 256:272], pbt[:, 272:288]
            kb.op("dve", lambda g: g.tensor_scalar(out=NACS, in0=acs_p, scalar1=-1.0, scalar2=None, op0=ALU.mult), reads=[pbtb], writes=[smb])
            kb.op("act", lambda g: g.activation(out=DOUT, in_=acs_p, func=AF.Exp), reads=[pbtb], writes=[smb])
            kb.op("dve", lambda g: g.tensor_tensor(out=TMP, in0=tot_p, in1=NACS, op=ALU.add), reads=[pbtb, smb], writes=[smb])
            kb.op("act", lambda g: g.activation(out=DSTS, in_=TMP, func=AF.Exp), reads=[smb], writes=[smb])
            kb.op("act", lambda g: g.activation(out=CD, in_=tot_p, func=AF.Exp), reads=[pbtb], writes=[smb])
            kb.op("dve", lambda g: g.tensor_tensor(out=h3(xdt[:]), in0=h3(xtm[:]), in1=bch(dt_j, 0, 16), op=ALU.mult),
                  reads=[xtmb, dwb], writes=[xdtb])
            kb.op("dve", lambda g: g.tensor_tensor(out=h3(xdd[:]), in0=h3(xdt[:]), in1=bch(DSTS, 0, 16), op=ALU.mult),
                  reads=[xdtb, smb], writes=[xddb])
            pcb, pcbb = bank(B_CB)
            for g4 in range(4):
                f = lambda g, g4=g4: g.matmul(pcb[:, g4 * 128:(g4 + 1) * 128], lhsT=xc[:, 8 + g4, js], rhs=xc[:, 12 + g4, js], start=True, stop=True)
                (kb.op if g4 == 3 else noinc)("pe", f, reads=[xall[8 + g4], xall[12 + g4]], writes=[pcbb])
            cbm, cbmb = cbm_r.get()
            kb.op("dve", lambda g: g.tensor_tensor(out=cbm[:].rearrange("p (g l) -> p g l", g=4), in0=pcb[:, :].rearrange("p (g l) -> p g l", g=4),
                                                   in1=U[:].unsqueeze(1).to_broadcast([128, 4, 128]), op=ALU.mult),
                  reads=[pcbb, cb], writes=[cbmb])
            for g4 in range(4):
                pR, pRb = bank(B_R if g4 % 2 == 0 else B_CV)
                noinc("pe", lambda g: g.matmul(pR[:, :], lhsT=ident[:], rhs=neg[:], start=True, stop=False), reads=[cb], writes=[pRb])
                for q in range(4):
                    h = 4 * g4 + q
                    f = lambda g, q=q, h=h: g.matmul(pR[:, q * 128:(q + 1) * 128], lhsT=a_j[:, h:h + 1].to_broadcast([128, 128]), rhs=U[:],
                                                     start=False, stop=(q == 3))
                    (kb.op if q == 3 else noinc)("pe", f, reads=[dwb, cb], writes=[pRb])
                Lm, Lb = L_r.get()
                kb.op("dve", lambda g: g.tensor_tensor(out=q3(Lm[:]), in0=q3(pR[:, :]), in1=NACS[:, 4 * g4:4 * g4 + 4].to_broadcast([128, 4, 128]),
                                                       op=ALU.add), reads=[pRb, smb], writes=[Lb])
                kb.op("act", lambda g: g.activation(out=Lm[:], in_=Lm[:], func=AF.Exp), reads=[Lb], writes=[Lb])
                kb.op("dve", lambda g: g.tensor_tensor(out=q3(Wall[:, g4 * 512:(g4 + 1) * 512]), in0=q3(Lm[:]),
                                                       in1=cbm[:, g4 * 128:(g4 + 1) * 128].unsqueeze(1).to_broadcast([128, 4, 128]),
                                                       op=ALU.mult), reads=[Lb, cbmb], writes=[Wbs[g4]])

        def Lt():
            yb, ybb = yb_r.get()
            for hh in range(2):
                py, pyb = bank(B_PY)
                po, pob = bank(B_PO)
                pst, pstb = bank(B_PS)
                for gq in range(2):
                    g4 = 2 * hh + gq
                    for q in range(4):
                        h = 4 * g4 + q
                        col = (h - 8 * hh) * 64
                        f = lambda g, h=h, col=col: g.matmul(py[:, col:col + 64], lhsT=Wall[:, h * 128:(h + 1) * 128], rhs=xdt[:, h * 64:(h + 1) * 64],
                                                             start=True, stop=True)
                        (kb.op if (q == 3 and gq == 1) else noinc)("pe", f, reads=[Wbs[g4], xdtb], writes=[pyb])
                for gq in range(2):
                    g4 = 2 * hh + gq
                    f = lambda g, gq=gq, g4=g4: g.matmul(po[:, gq * 256:(gq + 1) * 256], lhsT=xc[:, 12 + g4, js], rhs=prev_bf[:, g4 * 256:(g4 + 1) * 256],
                                                         start=True, stop=True)
                    (kb.op if gq == 1 else noinc)("pe", f, reads=[xall[12 + g4], pbb], writes=[pob])
                for gq in range(2):
                    g4 = 2 * hh + gq
                    f = lambda g, gq=gq, g4=g4: g.matmul(pst[:, gq * 256:(gq + 1) * 256], lhsT=btm[:, g4 * 128:(g4 + 1) * 128], rhs=xdd[:, g4 * 256:(g4 + 1) * 256],
                                                         start=True, stop=True)
                    (kb.op if gq == 1 else noinc)("pe", f, reads=[btmb, xddb], writes=[pstb])
                cs = slice(hh * 512, (hh + 1) * 512)
                kb.op("dve", lambda g: g.tensor_tensor(out=h3(yb[:, cs]), in0=h3(po[:, :]), in1=bch(DOUT, 8 * hh, 8), op=ALU.mult),
                      reads=[pob, smb], writes=[ybb])
                kb.op("dve", lambda g: g.tensor_tensor(out=yb[:, cs], in0=yb[:, cs], in1=py[:, :], op=ALU.add), reads=[pyb, ybb], writes=[ybb])
                kb.op("pool", lambda g: g.tensor_tensor(out=h3(ptmp[:, cs]), in0=h3(prev[:, cs]), in1=bch(CD, 8 * hh, 8), op=ALU.mult),
                      reads=[pvb, smb], writes=[ptb_])
                kb.op("dve", lambda g: g.tensor_tensor(out=prev[:, cs], in0=ptmp[:, cs], in1=pst[:, :], op=ALU.add), reads=[ptb_, pstb], writes=[pvb])
                kb.op("act", lambda g: g.copy(out=prev_bf[:, cs], in_=prev[:, cs]), reads=[pvb], writes=[pbb])
            t4, t4b = t4_r.get()
            kb.op("pool", lambda g: g.tensor_tensor(out=h3(t4[:]), in0=h3(xtm[:]), in1=bch(dsk, 0, 16), op=ALU.mult), reads=[xtmb, cb], writes=[t4b])
            kb.op("pool", lambda g: g.tensor_tensor(out=yb[:], in0=yb[:], in1=t4[:], op=ALU.add), reads=[t4b, ybb], writes=[ybb])
            kb.op("dve", lambda g: g.tensor_tensor(out=yb[:], in0=yb[:], in1=zc[:, j, :], op=ALU.mult), reads=[zcb, ybb], writes=[ybb])
            for g4 in range(4):
                jk, jkb = junk_r.get()
                kb.op("act", lambda g, g4=g4: g.activation(out=jk[:], in_=yb[:, g4 * 256:(g4 + 1) * 256], func=AF.Square, accum_out=SS[:, g4:g4 + 1]),
                      reads=[ybb], writes=[jkb, smb])
            kb.op("dve", lambda g: g.tensor_scalar(out=SS[:, 0:4], in0=SS[:, 0:4], scalar1=1.0 / 256.0, scalar2=EPS, op0=ALU.mult, op1=ALU.add),
                  reads=[smb], writes=[smb])
            kb.op("act", lambda g: g.activation(out=SS[:, 0:4], in_=SS[:, 0:4], func=AF.Ln), reads=[smb], writes=[smb])
            kb.op("act", lambda g: g.activation(out=RS[:, 0:4], in_=SS[:, 0:4], func=AF.Exp, scale=-0.5), reads=[smb], writes=[smb])
            kb.op("dve", lambda g: g.tensor_tensor(out=yb[:].rearrange("p (g e) -> p g e", g=4), in0=yb[:].rearrange("p (g e) -> p g e", g=4),
                                                   in1=RS[:, 0:4].to_broadcast([128, 4, 256]), op=ALU.mult), reads=[ybb, smb], writes=[ybb])
            yn, ynb = yn_r.get()
            kb.op("pool", lambda g: g.tensor_tensor(out=yn[:], in0=yb[:], in1=ng[:], op=ALU.mult), reads=[ybb, cb], writes=[ynb])
            pyt, pytb = bank(B_YT)
            pyv = pyt[:].bitcast(BF16)
            with kb.atomic():
                for c in range(8):
                    f = lambda g, c=c: g.transpose(out=pyv[:, c * 128:(c + 1) * 128], in_=yn[:, c * 128:(c + 1) * 128], identity=ident[:])
                    (kb.op if c == 7 else noinc)("pe", f, reads=[ynb, cb], writes=[pytb])
                kb.op("act", lambda g: g.copy(out=yTs[:, :, js], in_=pyv.rearrange("p (c t) -> p c t", c=8)), reads=[pytb], writes=[yTb])
            if kb.hook is not None:
                kb.hook()
            if j == 3:
                toks.append(kb.dma("pool", yT_o.rearrange("(c p) t -> p c t", p=128)[:, :, B["t0"]:B["t0"] + 512], yTs[:], reads=[yTb]))

        return E, Lt

    prevLt = None
    J = 0
    nblk = S // 512
    sc = SideCast(kb, side or [], 32)
    B = prologue(0)
    for tb in range(nblk):
        nxt = {}
        for j in range(4):
            E, Lt = make_chunk(B, j, J)
            pro = (lambda: nxt.update(B=prologue(tb + 1))) if (j == 1 and tb + 1 < nblk) else None
            interleave(kb, [E, prevLt, pro, sc.next()])
            prevLt = Lt
            J += 1
        B = nxt.get("B")
    interleave(kb, [prevLt, sc.next(last=True)])
    return toks


def m2_host_inputs(hf, xbcT_full, z_full, dt_full, conv_w, conv_b, dt_bias, a_log, d_skip, ssd_norm_g):
    ch = np.concatenate([np.arange(hf * 1024, (hf + 1) * 1024), 2048 + np.arange(hf * 512, (hf + 1) * 512),
                         3072 + np.arange(hf * 512, (hf + 1) * 512)])
    bc = lambda v_: np.ascontiguousarray(np.broadcast_to(np.asarray(v_, np.float32)[None, :], (128, len(v_))))
    cw = np.asarray(conv_w)[:, ch].reshape(4, 16, 128).transpose(2, 1, 0).reshape(128, 64)
    hs = slice(hf * 16, (hf + 1) * 16)
    full = xbcT_full.shape[0] == 4096 and xbcT_full.shape[1] > 4
    return {
        "xbcT": np.ascontiguousarray(xbcT_full[ch]) if full else None,
        "z": np.ascontiguousarray(z_full[:, hf * 1024:(hf + 1) * 1024]) if full else None,
        "dt": np.ascontiguousarray(dt_full[:, hs]) if full else None,
        "cw": np.ascontiguousarray(cw, dtype=np.float32),
        "cbias": np.ascontiguousarray(np.asarray(conv_b)[ch].reshape(16, 128).T, dtype=np.float32),
        "dtb": bc(np.asarray(dt_bias)[hs]), "alog": bc(np.asarray(a_log)[hs]), "dsk": bc(np.asarray(d_skip)[hs]),
        "ng": bc(np.asarray(ssd_norm_g)[hf * 1024:(hf + 1) * 1024]),
        "U": np.triu(np.ones((128, 128), np.float32)), "ident": np.eye(128, dtype=np.float32),
    }


def build_B1():
    kb = KB()
    kb.psum_banks()
    T = {"h1": kb.din("h1", [TPC, D], F32), "yattT": kb.din("yattT", [512, TPC], BF16), "yssdT": kb.din("yssdT", [2048, TPC], BF16),
         "gT": kb.din("gT", [2048, TPC], F32), "mem": kb.din("mem", [NMEM, D], F32), "ident": kb.din("ident", [128, 128], F32),
         "g4col": kb.din("g4col", [128, 8], F32), "mgcol": kb.din("mgcol", [128, 8], F32), "g3bc": kb.din("g3bc", [128, D], F32),
         "g5bc": kb.din("g5bc", [128, D], F32), "w_att": kb.din("w_att", [512, D], BF16), "w_ssd": kb.din("w_ssd", [2048, D], BF16),
         "w_o": kb.din("w_o", [D, D], BF16), "wq": kb.din("wq", [D, D], BF16), "wk": kb.din("wk", [D, D], BF16),
         "wv": kb.din("wv", [D, D], BF16), "wo": kb.din("wo", [D, D], BF16), "h3": kb.dout("h3", [TPC, D], F32)}
    emit_B1(kb, T, TPC)
    return kb.finish()


def emit_B1(kb, T, ntok):
    h1, yattT, yssdT, gT, mem, ident_d = T["h1"], T["yattT"], T["yssdT"], T["gT"], T["mem"], T["ident"]
    g4col_d, mgcol_d, g3bc_d, g5bc_d = T["g4col"], T["mgcol"], T["g3bc"], T["g5bc"]
    w_att, w_ssd, w_o, wq, wk, wv, wo, h3 = T["w_att"], T["w_ssd"], T["w_o"], T["wq"], T["wk"], T["wv"], T["wo"], T["h3"]
    toks = []
    dn = Dense(kb, ident_d)
    g4col = dn.load_const("g4col", [128, 8], g4col_d)
    mgcol = dn.load_const("mgcol", [128, 8], mgcol_d)
    g3bc = dn.load_const("g3bc", [128, D], g3bc_d)
    g5bc = dn.load_const("g5bc", [128, D], g5bc_d)
    ones_b = kb.sb("ones_b", [128, 128], BF16)
    kb.op("dve", lambda g: g.memset(ones_b[:], 1.0), writes=[dn.cb])
    wrot = Rot(kb, "wr", [128, 8, 1024], BF16, 3)
    hrot = Rot(kb, "hrot", [128, D], F32, 4)
    h2t = [(kb.sb("h2_%d" % i, [128, D], F32), Buf("h2_%d" % i)) for i in range(4)]
    ya = kb.sb("ya", [128, 4, 512], BF16)
    ys = kb.sb("ys", [128, 16, 512], BF16)
    yab, ysb = Buf("ya"), Buf("ys")
    grot = Rot(kb, "gr", [128, 512], F32, 4)
    m12 = Rot(kb, "m12", [128, 512], F32, 4)
    mT = kb.sb("mT", [128, 8, 512], BF16)
    mTb = [Buf("mT%d" % i) for i in range(8)]
    uT = kb.sb("uT", [128, 8, 512], BF16)
    ub = [Buf("uT%d" % i) for i in range(4)]
    memT = kb.sb("memT", [128, 8, 256], BF16)
    memTb = [Buf("memT0"), Buf("memT1")]
    KmT = kb.sb("KmT", [128, 8, 256], BF16)
    Vm = kb.sb("Vm", [128, 2, 1024], BF16)
    kmb, vmb = Buf("KmT"), Buf("Vm")
    QT = kb.sb("QT", [128, 8, 512], BF16)
    QTb = [Buf("QT%d" % i) for i in range(8)]
    OT = kb.sb("OT", [128, 8, 512], BF16)
    OTb = [Buf("OT%d" % i) for i in range(8)]
    PTr = Rot(kb, "PT", [128, 512], BF16, 4)
    rdr = Rot(kb, "rd", [128, 512], F32, 2)

    def loadw(w, kcn=8):
        t, b = wrot.get()
        kb.dma("sp", t[:, 0:kcn, :], wview(w)[:, 0:kcn, :], writes=[b])
        return t, b

    for mt in range(2):
        hi, hib = hrot.get()
        kb.dma("sp", hi[:], mem[mt * 128:(mt + 1) * 128, :], writes=[hib])
        dn.norm_T(hi[:], hib, mgcol[:, 0:8], memT[:, :, mt * 128:(mt + 1) * 128], memTb[mt])
    wkt, wkb = loadw(wk)
    for oc in range(8):
        pp, ppb = kb.ps()
        mm_group(kb, pp, ppb, pp[:, 0:256], [(wkt[:, kc, oc * 128:(oc + 1) * 128], memT[:, kc, :]) for kc in range(8)], reads=[wkb] + memTb)
        kb.op("act", lambda g: g.copy(out=KmT[:, oc, :], in_=pp[:, 0:256]), reads=[ppb], writes=[kmb])
    wvt, wvb = loadw(wv)
    for mt in range(2):
        for oh in range(2):
            pp, ppb = kb.ps()
            mm_group(kb, pp, ppb, pp[:, :], [(memT[:, kc, mt * 128:(mt + 1) * 128], wvt[:, kc, oh * 512:(oh + 1) * 512]) for kc in range(8)],
                     reads=[wvb] + memTb)
            kb.op("dve", lambda g: g.tensor_copy(out=Vm[:, mt, oh * 512:(oh + 1) * 512], in_=pp[:, :]), reads=[ppb], writes=[vmb])

    for p in range(ntok // 512):
        t0 = p * 512
        ts_ = slice(t0, t0 + 512)
        kb.dma("sp", ya[:], yattT.rearrange("(c p) t -> p c t", p=128)[:, :, ts_], writes=[yab])
        kb.dma("sp", ys[:, 0:8, :], yssdT.rearrange("(c p) t -> p c t", p=128)[:, 0:8, ts_], writes=[ysb])
        kb.dma("sp", ys[:, 8:16, :], yssdT.rearrange("(c p) t -> p c t", p=128)[:, 8:16, ts_], writes=[ysb])
        wa, wab = loadw(w_att, 4)
        ws0, ws0b = loadw(w_ssd[0:1024, :])
        ws1, ws1b = loadw(w_ssd[1024:2048, :])
        for oc in range(8):
            ocs = slice(oc * 128, (oc + 1) * 128)
            pa, pab = kb.ps()
            mm_group(kb, pa, pab, pa[:, :], [(wa[:, kc, ocs], ya[:, kc, :]) for kc in range(4)], reads=[wab, yab])
            pb_, pbb = kb.ps()
            mm_group(kb, pb_, pbb, pb_[:, :], [((ws0 if kc < 8 else ws1)[:, kc % 8, ocs], ys[:, kc, :]) for kc in range(16)],
                     reads=[ws0b, ws1b, ysb])
            g0, g0b = grot.get()
            g1, g1b = grot.get()
            kb.dma("sp", g0[:], gT[oc * 128:(oc + 1) * 128, ts_], writes=[g0b])
            kb.dma("sp", g1[:], gT[1024 + oc * 128:1024 + (oc + 1) * 128, ts_], writes=[g1b])
            ma, mab = m12.get()
            mb_, mbb = m12.get()
            kb.op("dve", lambda g: g.tensor_tensor(out=ma[:], in0=g0[:], in1=pa[:, :], op=ALU.mult), reads=[g0b, pab], writes=[mab])
            kb.op("dve", lambda g: g.tensor_tensor(out=mb_[:], in0=g1[:], in1=pb_[:, :], op=ALU.mult), reads=[g1b, pbb], writes=[mbb])
            kb.op("pool", lambda g: g.tensor_tensor(out=mT[:, oc, :], in0=ma[:], in1=mb_[:], op=ALU.add), reads=[mab, mbb], writes=[mTb[oc]])
        wot, wob = loadw(w_o)
        for t in range(4):
            halves = []
            for oh in range(2):
                py, pyb = kb.ps()
                mm_group(kb, py, pyb, py[:, :], [(mT[:, kc, t * 128:(t + 1) * 128], wot[:, kc, oh * 512:(oh + 1) * 512]) for kc in range(8)],
                         reads=mTb + [wob])
                halves.append((py[:, :], pyb))
            hi, hib = hrot.get()
            kb.dma("sp", hi[:], h1[t0 + t * 128:t0 + (t + 1) * 128, :], writes=[hib])
            dn.postnorm_res(halves, hi[:], hib, h2t[t][0][:], h2t[t][1], g3bc)
            dn.norm_T(h2t[t][0][:], h2t[t][1], g4col[:, 0:8], uT[:, :, t * 128:(t + 1) * 128], ub[t])
        wqt, wqb = loadw(wq)
        for oc in range(8):
            pp, ppb = kb.ps()
            mm_group(kb, pp, ppb, pp[:, :], [(wqt[:, kc, oc * 128:(oc + 1) * 128], uT[:, kc, :]) for kc in range(8)], reads=[wqb] + ub)
            if oc % 2 == 0:
                kb.op("act", lambda g: g.copy(out=QT[:, oc, :], in_=pp[:, :]), reads=[ppb], writes=[QTb[oc]])
            else:
                kb.op("dve", lambda g: g.tensor_copy(out=QT[:, oc, :], in_=pp[:, :]), reads=[ppb], writes=[QTb[oc]])
        for hx in range(4):
            PTs = []
            for mc in range(2):
                pS, pSb = kb.ps()
                mm_group(kb, pS, pSb, pS[:, :], [(KmT[:, 2 * hx + dc, mc * 128:(mc + 1) * 128], QT[:, 2 * hx + dc, :]) for dc in range(2)],
                         reads=[kmb, QTb[2 * hx], QTb[2 * hx + 1]])
                PT, PTb = PTr.get()
                kb.op("act", lambda g: g.activation(out=PT[:], in_=pS[:, :], func=AF.Exp, scale=1.0 / 16.0), reads=[pSb], writes=[PTb])
                PTs.append((PT, PTb))
            pD, pDb = kb.ps()
            mm_group(kb, pD, pDb, pD[:, :], [(ones_b[:], PTs[mc][0][:]) for mc in range(2)], reads=[dn.cb, PTs[0][1], PTs[1][1]])
            rd, rdb = rdr.get()
            kb.op("dve", lambda g: g.reciprocal(out=rd[:], in_=pD[:, :]), reads=[pDb], writes=[rdb])
            for dc in range(2):
                oc = 2 * hx + dc
                pO, pOb = kb.ps()
                mm_group(kb, pO, pOb, pO[:, :], [(Vm[:, mc, oc * 128:(oc + 1) * 128], PTs[mc][0][:]) for mc in range(2)],
                         reads=[vmb, PTs[0][1], PTs[1][1]])
                kb.op("dve", lambda g: g.tensor_tensor(out=OT[:, oc, :], in0=rd[:], in1=pO[:, :], op=ALU.mult), reads=[rdb, pOb], writes=[OTb[oc]])
        wxo, wxob = loadw(wo)
        for t in range(4):
            halves = []
            for oh in range(2):
                py, pyb = kb.ps()
                mm_group(kb, py, pyb, py[:, :], [(OT[:, kc, t * 128:(t + 1) * 128], wxo[:, kc, oh * 512:(oh + 1) * 512]) for kc in range(8)],
                         reads=OTb + [wxob])
                halves.append((py[:, :], pyb))
            ho, hob = hrot.get()
            dn.postnorm_res(halves, h2t[t][0][:], h2t[t][1], ho[:], hob, g5bc)
            toks.append(kb.dma("pool", h3[t0 + t * 128:t0 + (t + 1) * 128, :], ho[:], reads=[hob]))
    return toks


def _col(v_):
    return np.ascontiguousarray(np.asarray(v_, np.float32).reshape(-1, 128).T)


def _bc(v_):
    v_ = np.asarray(v_, np.float32)
    return np.ascontiguousarray(np.broadcast_to(v_[None, :], (128, v_.size)))


def _prog(name, fn):
    if name not in _cache:
        _cache[name] = fn()
    return _cache[name]


def kernel_unfused(x, mem, norm_g, ffn1_gate, ffn1_up, ffn1_down, w_in, b_gate, conv_w, conv_b,
           dt_bias, a_log, d_skip, ssd_norm_g, w_att_out, w_ssd_out, w_o, mem_norm_g,
           xa_wq, xa_wk, xa_wv, xa_wo, ffn2_gate, ffn2_up, ffn2_down):
    f32 = lambda a: np.asarray(a, np.float32)
    x, mem, norm_g = f32(x), f32(mem), f32(norm_g)
    ident = np.eye(128, dtype=np.float32)
    mask = attn_mask()
    h = np.ascontiguousarray(x.reshape(NCORES, TPC, D))
    hs = [h[c] for c in range(NCORES)]
    big = {"ffn1_gate": ffn1_gate, "ffn1_up": ffn1_up, "ffn1_down": ffn1_down, "w_in": w_in, "w_att_out": w_att_out,
           "w_ssd_out": w_ssd_out, "w_o": w_o, "xa_wq": xa_wq, "xa_wk": xa_wk, "xa_wv": xa_wv, "xa_wo": xa_wo,
           "ffn2_gate": ffn2_gate, "ffn2_up": ffn2_up, "ffn2_down": ffn2_down}
    for L in range(DEPTH):
        wb = cast_weights({k: f32(v_[L]) for k, v_ in big.items()})
        g = norm_g[L]
        ncA = _prog("A", lambda: build_A(True))
        resA = run(ncA, [{"h": hs[c], "ident": ident, "g0col": _col(g[0]), "g2col": _col(g[2]), "g1bc": _bc(g[1]),
                          "bgcol": _col(f32(b_gate[L])), "w_gate": wb["ffn1_gate"], "w_up": wb["ffn1_up"],
                          "w_down": wb["ffn1_down"], "w_in": wb["w_in"]} for c in range(NCORES)])
        cat_t = lambda name, b, ax: np.concatenate([np.asarray(resA[2 * b][name]), np.asarray(resA[2 * b + 1][name])], axis=ax)
        insM1, insM2 = [], []
        for b in range(BATCH):
            qk = cat_t("qkT", b, 1)
            vf = cat_t("v", b, 0)
            zf = cat_t("z", b, 0)
            xf = cat_t("xbcT", b, 1)
            df = cat_t("dt", b, 0)
            for hf in range(2):
                rows = np.concatenate([np.arange(gi * 512 + hf * 256, gi * 512 + hf * 256 + 256) for gi in range(3)])
                insM1.append({"qT": np.ascontiguousarray(qk[rows]), "kT": np.ascontiguousarray(qk[1536 + rows]),
                              "v": np.ascontiguousarray(vf[:, rows]), "mask": mask})
                insM2.append(m2_host_inputs(hf, xf, zf, df, f32(conv_w[L]), f32(conv_b[L]), f32(dt_bias[L]), f32(a_log[L]),
                                            f32(d_skip[L]), f32(ssd_norm_g[L])))
        resM1 = run(_prog("M1", build_M1), insM1)
        resM2 = run(_prog("M2", build_M2), insM2)
        insB = []
        for c in range(NCORES):
            b, half = c // 2, c % 2
            tsl = slice(half * TPC, (half + 1) * TPC)
            ya = np.concatenate([np.asarray(resM1[2 * b][("yattT")]), np.asarray(resM1[2 * b + 1]["yattT"])], axis=0)[:, tsl]
            ys = np.concatenate([np.asarray(resM2[2 * b]["yssdT"]), np.asarray(resM2[2 * b + 1]["yssdT"])], axis=0)[:, tsl]
            insB.append({"h1": np.asarray(resA[c]["h1"]), "yattT": np.ascontiguousarray(ya), "yssdT": np.ascontiguousarray(ys),
                         "gT": np.asarray(resA[c]["gT"]), "mem": np.ascontiguousarray(mem[b]), "ident": ident,
                         "g4col": _col(g[4]), "mgcol": _col(f32(mem_norm_g[L])), "g3bc": _bc(g[3]), "g5bc": _bc(g[5]),
                         "w_att": wb["w_att_out"], "w_ssd": wb["w_ssd_out"], "w_o": wb["w_o"], "wq": wb["xa_wq"],
                         "wk": wb["xa_wk"], "wv": wb["xa_wv"], "wo": wb["xa_wo"]})
        resB = run(_prog("B1", build_B1), insB)
        resF = run(_prog("B2", lambda: build_A(False)),
                   [{"h": np.asarray(resB[c]["h3"]), "ident": ident, "g0col": _col(g[6]), "g2col": _col(g[6]), "g1bc": _bc(g[7]),
                     "bgcol": _col(f32(b_gate[L])), "w_gate": wb["ffn2_gate"], "w_up": wb["ffn2_up"], "w_down": wb["ffn2_down"],
                     "w_in": wb["w_in"]} for c in range(NCORES)])
        hs = [np.asarray(resF[c]["h1"]) for c in range(NCORES)]
    out = np.stack(hs, axis=0).reshape(BATCH, SEQ, D).astype(np.float32)
    return out


BIGW = {"ffn1_gate": (D, DFF), "ffn1_up": (D, DFF), "ffn1_down": (DFF, D), "w_in": (D, INW), "w_att_out": (512, D),
        "w_ssd_out": (2048, D), "w_o": (D, D), "xa_wq": (D, D), "xa_wk": (D, D), "xa_wv": (D, D), "xa_wo": (D, D),
        "ffn2_gate": (D, DFF), "ffn2_up": (D, DFF), "ffn2_down": (DFF, D)}


def build_fused(depth=DEPTH):
    kb = KB()
    kb.psum_banks()
    S = SEQ
    x = kb.din("x", [S, D], F32)
    mem = kb.din("mem", [NMEM, D], F32)
    ident = kb.din("ident", [128, 128], F32)
    mask = kb.din("mask", [128, 512], F32)
    U = kb.din("U", [128, 128], F32)
    gcol = kb.din("gcol", [DEPTH * 8, 128, 8], F32)
    gbc = kb.din("gbc", [DEPTH * 8, 128, D], F32)
    bgcol = kb.din("bgcol", [DEPTH, 128, 16], F32)
    mgcol = kb.din("mgcol", [DEPTH, 128, 8], F32)
    cw = kb.din("cw", [DEPTH * 2, 128, 64], F32)
    cbias = kb.din("cbias", [DEPTH * 2, 128, 16], F32)
    dtb = kb.din("dtb", [DEPTH * 2, 128, 16], F32)
    alog = kb.din("alog", [DEPTH * 2, 128, 16], F32)
    dsk = kb.din("dsk", [DEPTH * 2, 128, 16], F32)
    ng = kb.din("ng", [DEPTH * 2, 128, 1024], F32)
    wf = {n: kb.din(n, [DEPTH, k, m], F32) for n, (k, m) in BIGW.items()}
    wb = {n: kb.dint(n + "_bf", [DEPTH, k, m], BF16) for n, (k, m) in BIGW.items()}
    out = kb.dout("out", [S, D], F32)
    qkT = kb.dint("s_qkT", [3072, S], BF16)
    v = kb.dint("s_v", [S, 1536], BF16)
    z = kb.dint("s_z", [S, 2048], F32)
    xbcT = kb.dint("s_xbcT", [4096, S], F32)
    dt = kb.dint("s_dt", [S, 32], F32)
    gT = kb.dint("s_gT", [2048, S], F32)
    yattT = kb.dint("s_yattT", [512, S], BF16)
    yssdT = kb.dint("s_yssdT", [2048, S], BF16)
    u2T = kb.dint("s_u2T", [D, S], BF16)
    hA = kb.dint("s_hA", [S, D], F32)
    hB = kb.dint("s_hB", [S, D], F32)
    hC = kb.dint("s_hC", [S, D], F32)

    def flat(ap):
        return ap.rearrange("k n -> (k n)").rearrange("(p f) -> p f", p=128)

    for n, (k, m) in BIGW.items():
        for L0 in range(depth):
            emit_cast(kb, flat(wf[n][L0]), flat(wb[n][L0]), (k * m) // 128)
            kb.end_phase()
            kb.begin_phase()

    def cast_pieces(L, CH=1536):
        out_ = []
        for n, (k, m) in BIGW.items():
            N = (k * m) // 128
            sv, dv = flat(wf[n][L]), flat(wb[n][L])
            for lo in range(0, N, CH):
                w_ = min(CH, N - lo)
                out_.append((sv[:, lo:lo + w_], dv[:, lo:lo + w_], w_))
        return out_

    hin = x
    for L in range(depth):
        TA = {"h": hin, "ident": ident, "g0col": gcol[L * 8 + 0], "g2col": gcol[L * 8 + 2], "g1bc": gbc[L * 8 + 1], "bgcol": bgcol[L],
              "w_gate": wb["ffn1_gate"][L], "w_up": wb["ffn1_up"][L], "w_down": wb["ffn1_down"][L], "w_in": wb["w_in"][L], "h1": hA,
              "qkT": qkT, "v": v, "z": z, "xbcT": xbcT, "dt": dt, "gT": gT, "u2T": u2T}
        emit_A(kb, TA, S, True)
        kb.end_phase()
        pieces = []
        half_ = (len(pieces) + 1) // 2
        for hf in range(2):
            kb.begin_phase()
            emit_M1(kb, lambda g, hf=hf: qkT[g * 512 + hf * 256:g * 512 + hf * 256 + 256, :],
                    lambda g, hf=hf: qkT[1536 + g * 512 + hf * 256:1536 + g * 512 + hf * 256 + 256, :],
                    lambda g, hf=hf: v[:, g * 512 + hf * 256:g * 512 + hf * 256 + 256], mask, yattT[hf * 256:(hf + 1) * 256, :],
                    side=None)
            kb.end_phase()
        for hf in range(2):
            kb.begin_phase()

            def xrow(c, hf=hf):
                if c < 8:
                    r0 = hf * 1024 + c * 128
                elif c < 12:
                    r0 = 2048 + hf * 512 + (c - 8) * 128
                else:
                    r0 = 3072 + hf * 512 + (c - 12) * 128
                return xbcT[r0:r0 + 128, :]

            i2 = L * 2 + hf
            TM = {"xrow": xrow, "z": z[:, hf * 1024:(hf + 1) * 1024], "dt": dt[:, hf * 16:(hf + 1) * 16], "cw": cw[i2], "cbias": cbias[i2],
                  "dtb": dtb[i2], "alog": alog[i2], "dsk": dsk[i2], "ng": ng[i2], "U": U, "ident": ident,
                  "yT": yssdT[hf * 1024:(hf + 1) * 1024, :]}
            emit_M2(kb, TM, side=pieces[hf * half_:(hf + 1) * half_])
            kb.end_phase()
        kb.begin_phase()
        TB = {"h1": hA, "yattT": yattT, "yssdT": yssdT, "gT": gT, "mem": mem, "ident": ident, "g4col": gcol[L * 8 + 4], "mgcol": mgcol[L],
              "g3bc": gbc[L * 8 + 3], "g5bc": gbc[L * 8 + 5], "w_att": wb["w_att_out"][L], "w_ssd": wb["w_ssd_out"][L], "w_o": wb["w_o"][L],
              "wq": wb["xa_wq"][L], "wk": wb["xa_wk"][L], "wv": wb["xa_wv"][L], "wo": wb["xa_wo"][L], "h3": hB}
        emit_B1(kb, TB, S)
        kb.end_phase()
        kb.begin_phase()
        hout = out if L == depth - 1 else hC
        TF = {"h": hB, "ident": ident, "g0col": gcol[L * 8 + 6], "g2col": gcol[L * 8 + 6], "g1bc": gbc[L * 8 + 7], "bgcol": bgcol[L],
              "w_gate": wb["ffn2_gate"][L], "w_up": wb["ffn2_up"][L], "w_down": wb["ffn2_down"][L], "w_in": wb["w_in"][L], "h1": hout}
        emit_A(kb, TF, S, False)
        if L < depth - 1:
            kb.end_phase()
            kb.begin_phase()
        hin = hC
    return kb.finish()


def fused_inputs(x_b, mem_b, P):
    f32 = lambda a: np.ascontiguousarray(np.asarray(a, np.float32))
    m = {"x": f32(x_b), "mem": f32(mem_b), "ident": np.eye(128, dtype=np.float32), "mask": attn_mask(),
         "U": np.triu(np.ones((128, 128), np.float32))}
    ngm = np.asarray(P["norm_g"], np.float32)
    m["gcol"] = np.ascontiguousarray(ngm.reshape(DEPTH * 8, 8, 128).transpose(0, 2, 1))
    m["gbc"] = np.ascontiguousarray(np.broadcast_to(ngm.reshape(DEPTH * 8, 1, D), (DEPTH * 8, 128, D)))
    m["bgcol"] = np.ascontiguousarray(np.asarray(P["b_gate"], np.float32).reshape(DEPTH, 16, 128).transpose(0, 2, 1))
    m["mgcol"] = np.ascontiguousarray(np.asarray(P["mem_norm_g"], np.float32).reshape(DEPTH, 8, 128).transpose(0, 2, 1))
    cw, cbias, dtb, alog, dsk, ng = [], [], [], [], [], []
    dummy = np.zeros((4096, 4), np.float32)
    for L in range(DEPTH):
        for hf in range(2):
            r = m2_host_inputs(hf, dummy, dummy[:, :0].reshape(4096, 0) if False else np.zeros((4, 2048), np.float32),
                               np.zeros((4, 32), np.float32), P["conv_w"][L], P["conv_b"][L], P["dt_bias"][L], P["a_log"][L],
                               P["d_skip"][L], P["ssd_norm_g"][L])
            cw.append(r["cw"]); cbias.append(r["cbias"]); dtb.append(r["dtb"]); alog.append(r["alog"]); dsk.append(r["dsk"]); ng.append(r["ng"])
    for k_, lst in (("cw", cw), ("cbias", cbias), ("dtb", dtb), ("alog", alog), ("dsk", dsk), ("ng", ng)):
        m[k_] = np.ascontiguousarray(np.stack(lst, axis=0), dtype=np.float32)
    for n in BIGW:
        m[n] = f32(P[n])
    return m


def kernel(x, mem, norm_g, ffn1_gate, ffn1_up, ffn1_down, w_in, b_gate, conv_w, conv_b,
           dt_bias, a_log, d_skip, ssd_norm_g, w_att_out, w_ssd_out, w_o, mem_norm_g,
           xa_wq, xa_wk, xa_wv, xa_wo, ffn2_gate, ffn2_up, ffn2_down):
    P = dict(norm_g=norm_g, ffn1_gate=ffn1_gate, ffn1_up=ffn1_up, ffn1_down=ffn1_down, w_in=w_in, b_gate=b_gate, conv_w=conv_w,
             conv_b=conv_b, dt_bias=dt_bias, a_log=a_log, d_skip=d_skip, ssd_norm_g=ssd_norm_g, w_att_out=w_att_out,
             w_ssd_out=w_ssd_out, w_o=w_o, mem_norm_g=mem_norm_g, xa_wq=xa_wq, xa_wk=xa_wk, xa_wv=xa_wv, xa_wo=xa_wo,
             ffn2_gate=ffn2_gate, ffn2_up=ffn2_up, ffn2_down=ffn2_down)
    P = {k_: np.asarray(v_, np.float32) for k_, v_ in P.items()}
    x = np.asarray(x, np.float32)
    mem = np.asarray(mem, np.float32)
    nc = _prog("fused", build_fused)
    base = fused_inputs(x[0], mem[0], P)
    zero = {k_: np.zeros_like(v_) for k_, v_ in base.items()}
    in_maps = [zero] * NCORES
    for b in range(BATCH):
        mb_ = dict(base)
        mb_["x"] = np.ascontiguousarray(x[b])
        mb_["mem"] = np.ascontiguousarray(mem[b])
        in_maps[WORK_CORES[b]] = mb_
    res = run_bass_kernel_spmd(nc, in_maps, core_ids=list(range(NCORES)))
    return np.stack([np.asarray(res.results[WORK_CORES[b]]["out"], np.float32) for b in range(BATCH)], axis=0)
```
